# Optimizing a Trainium2 kernel written in Bass

```python
import jax, jax.numpy as jnp
from jax import lax
import numpy as np

D_MODEL = 1024
BATCH = 8
SEQ = 4096
DEPTH = 2

ATTN_HEADS = 8
ATTN_KV_HEADS = 2
ATTN_HEAD_DIM = 64
WINDOW = 128
GDN_HEADS = 4
GDN_HEAD_DIM = 128
GDN_CHUNK = 64
GDN_CONV = 4
D_FF = 2816
FFN_CONV = 3
NORM_EPS = 1e-6

ATTN_Q_DIM = ATTN_HEADS * ATTN_HEAD_DIM
ATTN_KV_DIM = ATTN_KV_HEADS * ATTN_HEAD_DIM
GDN_DIM = GDN_HEADS * GDN_HEAD_DIM
MIX_DIM = ATTN_Q_DIM + GDN_DIM
IN_SPLITS = (ATTN_Q_DIM, ATTN_KV_DIM, ATTN_KV_DIM, 3 * GDN_DIM, GDN_DIM, GDN_HEADS, GDN_HEADS)
IN_DIM = sum(IN_SPLITS)

kernel_name = "hybrid_swa_sink_alibi_gdn_convffn"


def rmsnorm(x, w):
    xf = x.astype(jnp.float32)
    y = xf * lax.rsqrt(jnp.mean(xf * xf, axis=-1, keepdims=True) + NORM_EPS)
    return (y * w.astype(jnp.float32)).astype(x.dtype)


def l2norm(x):
    return x * lax.rsqrt(jnp.sum(x * x, axis=-1, keepdims=True) + NORM_EPS)


def causal_dwconv(x, w):
    K, C = w.shape
    return lax.conv_general_dilated(
        x, w.astype(x.dtype)[:, None, :], window_strides=(1,), padding=[(K - 1, 0)],
        dimension_numbers=("NWC", "WIO", "NWC"), feature_group_count=C)


def alibi_slopes(n):
    return 2.0 ** (-8.0 * jnp.arange(1, n + 1, dtype=jnp.float32) / n)


def sliding_window_attention(q, k, v, sinks):
    B, T, Hq, Dh = q.shape
    Hkv = k.shape[2]
    G = Hq // Hkv
    W = WINDOW
    NB = T // W
    qb = q.reshape(B, NB, W, Hkv, G, Dh)

    def with_prev(x):
        xb = x.reshape(B, NB, W, Hkv, Dh)
        prev = jnp.pad(xb[:, :-1], ((0, 0), (1, 0), (0, 0), (0, 0), (0, 0)))
        return jnp.concatenate([prev, xb], axis=2)

    kb, vb = with_prev(k), with_prev(v)
    s = jnp.einsum("bnqhgd,bnkhd->bnhgqk", qb, kb).astype(jnp.float32) * (Dh ** -0.5)
    qpos = jnp.arange(W)[:, None] + W
    kpos = jnp.arange(2 * W)[None, :]
    rel = qpos - kpos
    band = (rel >= 0) & (rel < W)
    blk = jnp.arange(NB)[:, None, None]
    valid = band[None] & ((blk > 0) | (kpos >= W)[None])
    slopes = alibi_slopes(Hq).reshape(Hkv, G)
    alibi = -slopes[:, :, None, None] * rel.astype(jnp.float32)[None, None]
    s = jnp.where(valid[None, :, None, None], s + alibi, -jnp.inf)
    sink = jnp.broadcast_to(sinks.astype(jnp.float32).reshape(Hkv, G)[:, :, None, None], s.shape[:-1] + (1,))
    p = jax.nn.softmax(jnp.concatenate([s, sink], axis=-1), axis=-1)[..., :-1]
    o = jnp.einsum("bnhgqk,bnkhd->bnqhgd", p.astype(v.dtype), vb)
    return o.reshape(B, T, Hq * Dh)


def gated_delta_rule_chunked(q, k, v, g, beta):
    B, T, H, Dk = q.shape
    Dv = v.shape[-1]
    C = GDN_CHUNK
    N = T // C
    q = q * (Dk ** -0.5)

    def chunkify(x):
        return x.reshape(B, N, C, H, -1).transpose(0, 3, 1, 2, 4)

    qc, kc, vc = chunkify(q), chunkify(k), chunkify(v)
    gc = g.reshape(B, N, C, H).transpose(0, 3, 1, 2)
    bc = beta.reshape(B, N, C, H).transpose(0, 3, 1, 2)
    Gc = jnp.cumsum(gc, axis=-1)
    causal = jnp.tril(jnp.ones((C, C), dtype=bool))
    strict = jnp.tril(jnp.ones((C, C), dtype=bool), -1)
    decay = jnp.exp(jnp.where(causal, Gc[..., :, None] - Gc[..., None, :], -jnp.inf))
    kbeta = kc * bc[..., None]
    A = jnp.where(strict, jnp.einsum("bhncd,bhnsd->bhncs", kbeta, kc) * decay, 0.0)
    rhs = jnp.concatenate([vc * bc[..., None], kbeta * jnp.exp(Gc)[..., None]], axis=-1)
    sol = lax.linalg.triangular_solve(A, rhs, left_side=True, lower=True, unit_diagonal=True)
    u = sol[..., :Dv]
    w = sol[..., Dv:]
    qk = jnp.einsum("bhncd,bhnsd->bhncs", qc, kc) * decay
    qd = qc * jnp.exp(Gc)[..., None]
    kd = kc * jnp.exp(Gc[..., -1:] - Gc)[..., None]
    glast = jnp.exp(Gc[..., -1])

    def step(S, xs):
        qk_i, qd_i, w_i, u_i, kd_i, gl_i = xs
        v_new = u_i - jnp.einsum("bhck,bhkv->bhcv", w_i, S)
        o_i = jnp.einsum("bhck,bhkv->bhcv", qd_i, S) + jnp.einsum("bhcs,bhsv->bhcv", qk_i, v_new)
        S = S * gl_i[..., None, None] + jnp.einsum("bhck,bhcv->bhkv", kd_i, v_new)
        return S, o_i

    xs = tuple(jnp.moveaxis(a, 2, 0) for a in (qk, qd, w, u, kd, glast))
    S0 = jnp.zeros((B, H, Dk, Dv), jnp.float32)
    _, o = lax.scan(step, S0, xs)
    return o.transpose(1, 0, 3, 2, 4).reshape(B, T, H, Dv)


def gated_deltanet(qkv, z, b, a, conv_w, a_log, dt_bias, norm_w):
    B, T, _ = qkv.shape
    qkv = jax.nn.silu(causal_dwconv(qkv, conv_w)).astype(jnp.float32)
    qg, kg, vg = jnp.split(qkv, 3, axis=-1)
    qg = l2norm(qg.reshape(B, T, GDN_HEADS, GDN_HEAD_DIM))
    kg = l2norm(kg.reshape(B, T, GDN_HEADS, GDN_HEAD_DIM))
    vg = vg.reshape(B, T, GDN_HEADS, GDN_HEAD_DIM)
    beta = jax.nn.sigmoid(b.astype(jnp.float32))
    g = -jnp.exp(a_log.astype(jnp.float32)) * jax.nn.softplus(a.astype(jnp.float32) + dt_bias.astype(jnp.float32))
    o = gated_delta_rule_chunked(qg, kg, vg, g, beta)
    o = o * lax.rsqrt(jnp.mean(o * o, axis=-1, keepdims=True) + NORM_EPS) * norm_w.astype(jnp.float32)
    o = o * jax.nn.silu(z.astype(jnp.float32).reshape(B, T, GDN_HEADS, GDN_HEAD_DIM))
    return o.reshape(B, T, GDN_DIM).astype(z.dtype)


def conv_gated_mlp(h, w_in, conv_w, conv_b, w_down):
    gate, up = jnp.split(h @ w_in, 2, axis=-1)
    gate = causal_dwconv(gate, conv_w) + conv_b
    return (jax.nn.silu(gate) * up) @ w_down


def setup_inputs(seed: int = 0) -> dict:
    key = jax.random.key(seed)
    ks = jax.random.split(key, 16)
    f32 = jnp.float32
    nrm = lambda k, s, sc: jax.random.normal(k, s, f32) * sc
    dt = jnp.exp(jax.random.uniform(ks[6], (DEPTH, GDN_HEADS), f32, np.log(1e-3), np.log(1e-1)))
    return {
        "x": jax.random.normal(ks[0], (BATCH, SEQ, D_MODEL), f32),
        "attn_norm": 1.0 + nrm(ks[1], (DEPTH, D_MODEL), 0.02),
        "w_in": nrm(ks[2], (DEPTH, D_MODEL, IN_DIM), D_MODEL ** -0.5),
        "attn_sinks": nrm(ks[3], (DEPTH, ATTN_HEADS), 0.5),
        "gdn_conv_w": nrm(ks[4], (DEPTH, GDN_CONV, 3 * GDN_DIM), GDN_CONV ** -0.5),
        "gdn_a_log": jnp.log(jax.random.uniform(ks[5], (DEPTH, GDN_HEADS), f32, 1.0, 16.0)),
        "gdn_dt_bias": dt + jnp.log(-jnp.expm1(-dt)),
        "gdn_norm": 1.0 + nrm(ks[7], (DEPTH, GDN_HEAD_DIM), 0.02),
        "w_out": nrm(ks[8], (DEPTH, MIX_DIM, D_MODEL), MIX_DIM ** -0.5),
        "ffn_norm": 1.0 + nrm(ks[9], (DEPTH, D_MODEL), 0.02),
        "w_ffn_in": nrm(ks[10], (DEPTH, D_MODEL, 2 * D_FF), D_MODEL ** -0.5),
        "ffn_conv_w": nrm(ks[11], (DEPTH, FFN_CONV, D_FF), FFN_CONV ** -0.5),
        "ffn_conv_b": nrm(ks[12], (DEPTH, D_FF), 0.01),
        "w_down": nrm(ks[13], (DEPTH, D_FF, D_MODEL), D_FF ** -0.5),
        "final_norm": 1.0 + nrm(ks[14], (D_MODEL,), 0.02),
    }


def reference(x, attn_norm, w_in, attn_sinks, gdn_conv_w, gdn_a_log, gdn_dt_bias, gdn_norm,
              w_out, ffn_norm, w_ffn_in, ffn_conv_w, ffn_conv_b, w_down, final_norm):
    B, T, _ = x.shape
    cuts = np.cumsum(IN_SPLITS)[:-1].tolist()
    for l in range(DEPTH):
        h = rmsnorm(x, attn_norm[l])
        qa, ka, va, qkv_g, z, b, a = jnp.split(h @ w_in[l], cuts, axis=-1)
        attn_out = sliding_window_attention(
            qa.reshape(B, T, ATTN_HEADS, ATTN_HEAD_DIM),
            ka.reshape(B, T, ATTN_KV_HEADS, ATTN_HEAD_DIM),
            va.reshape(B, T, ATTN_KV_HEADS, ATTN_HEAD_DIM),
            attn_sinks[l])
        gdn_out = gated_deltanet(qkv_g, z, b, a, gdn_conv_w[l], gdn_a_log[l], gdn_dt_bias[l], gdn_norm[l])
        mixed = jnp.concatenate([attn_out.astype(x.dtype), gdn_out.astype(x.dtype)], axis=-1)
        x = x + mixed @ w_out[l]
        x = x + conv_gated_mlp(rmsnorm(x, ffn_norm[l]), w_ffn_in[l], ffn_conv_w[l], ffn_conv_b[l], w_down[l])
    return rmsnorm(x, final_norm)
```

```python
import contextlib
import os
import numpy as np
import concourse.bass as bass
import concourse.mybir as mybir
from concourse.bass_utils import run_bass_kernel_spmd

F32 = mybir.dt.float32
BF16 = mybir.dt.bfloat16
AF = mybir.ActivationFunctionType
ALU = mybir.AluOpType

ENGS = ("pe", "act", "dve", "pool", "sp")
DMA_K = 8

D = 1024
TT = 512
NB = TT // 128
IN_DIM = 2824
D_FF = 2816
NFF = D_FF // 128
EPS = 1e-6
BIG = 30000.0
SEQ = 4096
DEPTH = 2
NCORES = 8
STAGE = int(os.environ.get("KSTAGE", "99"))
SUB = int(os.environ.get("KSUB", "99"))


class Op:
    __slots__ = ("eng", "fn", "reads", "writes", "dma", "deps", "signal", "semval", "dslot", "gi")

    def __init__(self, eng, fn, reads, writes, dma):
        self.eng, self.fn, self.reads, self.writes, self.dma = eng, fn, reads, writes, dma
        self.deps = []
        self.signal = False
        self.semval = 0
        self.dslot = None


def _norm(keys):
    out = []
    for k in keys:
        if k is None:
            continue
        out.append(k if isinstance(k, tuple) else (k,))
    return out


class Prog:
    def __init__(self, nc):
        self.nc = nc
        self.ops = []
        self.state = {}
        self._cap = None

    def capture(self, body):
        assert self._cap is None
        self._cap = []
        body()
        lane, self._cap = self._cap, None
        return lane

    def replay(self, lane):
        for item in lane:
            self.add(*item)

    def add(self, eng, fn, reads=(), writes=(), dma=False):
        if self._cap is not None:
            self._cap.append((eng, fn, reads, writes, dma))
            return None
        op = Op(eng, fn, _norm(reads), _norm(writes), dma)
        op.gi = len(self.ops)
        deps = set()
        for k in op.reads:
            for kk, st in self._conf(k):
                if st[0] is not None:
                    deps.add((0, st[0]))
                if k[0].startswith("ps"):
                    for r in st[1]:
                        if self.ops[r].eng != eng:
                            deps.add((0, r))
        for k in op.writes:
            for kk, st in self._conf(k):
                if st[0] is not None:
                    deps.add((0, st[0]))
                for r in st[1]:
                    deps.add((1, r))
        final = set()
        for war, d in deps:
            if d == op.gi:
                continue
            a = self.ops[d]
            if a.eng == op.eng and not a.dma:
                if op.eng == "pe":
                    continue
            final.add(d)
        op.deps = sorted(final)
        for d in op.deps:
            self.ops[d].signal = True
        for k in op.reads:
            self._get(k)[1].append(op.gi)
        for k in op.writes:
            dd = self.state.setdefault(k[0], {})
            for kk in list(dd.keys()):
                if kk[: len(k)] == k and kk != k:
                    del dd[kk]
            dd[k] = [op.gi, []]
        self.ops.append(op)
        return op

    def _get(self, k):
        dd = self.state.setdefault(k[0], {})
        if k not in dd:
            par = None
            for kk, st in dd.items():
                if k[: len(kk)] == kk:
                    par = st
            dd[k] = [par[0] if par else None, []]
        return dd[k]

    def _conf(self, k):
        dd = self.state.get(k[0], {})
        for kk, st in dd.items():
            n = min(len(k), len(kk))
            if k[:n] == kk[:n]:
                yield kk, st

    def emit(self, final_wait_eng="sp"):
        nc = self.nc
        per = {e: [] for e in ENGS}
        for op in self.ops:
            per[op.eng].append(op)
        for e in ENGS:
            c = 0
            nd = 0
            for op in per[e]:
                if op.dma:
                    op.dslot = nd
                    nd += 1
                    op.signal = True
                elif op.signal:
                    c += 1
                    op.semval = c
        stack = contextlib.ExitStack()
        with stack:
            csem = {e: stack.enter_context(nc.semaphore("c_" + e)) for e in ENGS if e != "sp"}
            dsem = {e: [stack.enter_context(nc.semaphore("d_%s%d" % (e, i))) for i in range(DMA_K)]
                    for e in ("sp", "pool")}
            block = stack.enter_context(nc.Block())
            ops = self.ops

            def gen(e, engobj):
                waited = {}
                for op in per[e]:
                    need = {}
                    for d in op.deps:
                        a = ops[d]
                        if a.dma:
                            s = dsem[a.eng][a.dslot % DMA_K]
                            v = 16 * (a.dslot // DMA_K + 1)
                        else:
                            s = csem[a.eng]
                            v = a.semval
                        key = id(s)
                        if need.get(key, (None, 0))[1] < v:
                            need[key] = (s, v)
                    if op.dma and op.dslot >= DMA_K:
                        s = dsem[e][op.dslot % DMA_K]
                        v = 16 * (op.dslot // DMA_K)
                        key = id(s)
                        if need.get(key, (None, 0))[1] < v:
                            need[key] = (s, v)
                    for key, (s, v) in need.items():
                        if waited.get(key, 0) >= v:
                            continue
                        engobj.wait_ge(s, v)
                        waited[key] = v
                    ins = op.fn(engobj)
                    if op.dma:
                        ins.then_inc(dsem[e][op.dslot % DMA_K], 16)
                    elif op.signal:
                        ins.then_inc(csem[e], 1)
                if e == final_wait_eng:
                    for q in ("sp", "pool"):
                        n = len([o for o in per[q] if o.dma])
                        for i in range(min(DMA_K, n)):
                            cnt = (n - 1 - i) // DMA_K + 1
                            engobj.wait_ge(dsem[q][i], 16 * cnt)

            @block.tensor
            def _(t):
                gen("pe", t)

            @block.scalar
            def _(t):
                gen("act", t)

            @block.vector
            def _(t):
                gen("dve", t)

            @block.gpsimd
            def _(t):
                gen("pool", t)

            @block.sync
            def _(t):
                gen("sp", t)


def merge_lanes(lanes):
    lanes = [l for l in lanes if l]
    idx = [0] * len(lanes)
    out = []
    total = sum(len(l) for l in lanes)
    while len(out) < total:
        best = min((i for i in range(len(lanes)) if idx[i] < len(lanes[i])), key=lambda i: (idx[i] + 0.5) / len(lanes[i]))
        out.append(lanes[best][idx[best]])
        idx[best] += 1
    return out


def host_consts():
    ident = np.eye(128, dtype=np.float32)
    i = np.arange(128)
    triu = (i[:, None] <= i[None, :]).astype(np.float32)
    m1 = np.where(i[None, :] >= i[:, None], BIG, 0.0).astype(np.float32)
    m3 = np.where(i[None, :] < i[:, None], -BIG, 0.0).astype(np.float32)
    slopes = 2.0 ** (-8.0 * np.arange(1, 9) / 8.0)
    eb = np.zeros((128, 8, 256), np.float32)
    kk = i[:, None]
    qq = i[None, :]
    for h in range(8):
        relp = qq + 128 - kk
        eb[:, h, 0:128] = np.where(kk > qq, np.exp(-slopes[h] * relp), 0.0)
        relc = qq - kk
        eb[:, h, 128:256] = np.where(kk <= qq, np.exp(-slopes[h] * relc), 0.0)
    return {"c_ident": ident, "c_triu": triu, "c_m1": m1, "c_m3": m3, "c_eb": eb.reshape(128, 2048)}


def build_program(ntiles, depth, dbg=False):
    T = ntiles * TT
    nc = bass.Bass("TRN2", target_bir_lowering=False)
    din = lambda n, s: nc.dram_tensor(n, s, F32, kind="ExternalInput").ap()
    x_d = din("x", [T, D])
    attn_norm_d = din("attn_norm", [depth, D])
    w_in_d = din("w_in", [depth, D, IN_DIM])
    sinks_d = din("attn_sinks", [depth, 8])
    gconv_d = din("gdn_conv_w", [depth, 4, 1536])
    alog_d = din("gdn_a_log", [depth, 4])
    dtb_d = din("gdn_dt_bias", [depth, 4])
    gnorm_d = din("gdn_norm", [depth, 128])
    w_out_d = din("w_out", [depth, D, D])
    ffn_norm_d = din("ffn_norm", [depth, D])
    w_ffn_d = din("w_ffn_in", [depth, D, 2 * D_FF])
    fconv_d = din("ffn_conv_w", [depth, 3, D_FF])
    fbias_d = din("ffn_conv_b", [depth, D_FF])
    w_down_d = din("w_down", [depth, D_FF, D])
    fnorm_d = din("final_norm", [D])
    c_ident = din("c_ident", [128, 128])
    c_triu = din("c_triu", [128, 128])
    c_m1 = din("c_m1", [128, 128])
    c_m3 = din("c_m3", [128, 128])
    c_eb = din("c_eb", [128, 2048])
    out_d = nc.dram_tensor("out", [T, D], F32, kind="ExternalOutput").ap()
    dsc = lambda n, s_: nc.dram_tensor(n, s_, BF16, kind="Internal").ap()
    wsc_in = [dsc("wsc_in%d" % l, [D, IN_DIM]) for l in range(depth)]
    wsc_out = [dsc("wsc_out%d" % l, [D, D]) for l in range(depth)]
    wsc_ffn = [dsc("wsc_ffn%d" % l, [D, 2 * D_FF]) for l in range(depth)]
    wsc_dn = [dsc("wsc_dn%d" % l, [D_FF, D]) for l in range(depth)]
    dbg_d = {}
    if dbg:
        dbg_d["mixed"] = nc.dram_tensor("dbg_mixed", [D, TT], F32, kind="ExternalOutput").ap()
        dbg_d["oT"] = nc.dram_tensor("dbg_oT", [512, TT], F32, kind="ExternalOutput").ap()
        dbg_d["xmid"] = nc.dram_tensor("dbg_xmid", [D, TT], F32, kind="ExternalOutput").ap()
        dbg_d["xout"] = nc.dram_tensor("dbg_xout", [D, TT], F32, kind="ExternalOutput").ap()

    es = contextlib.ExitStack()
    with es, nc.allow_low_precision("bf16 matmul operands with fp32 PSUM accumulation"):
        sb = lambda n, s, d: es.enter_context(nc.sbuf_tensor(n, s, d))
        P = Prog(nc)

        def MM(out, lhsT, rhs, start, stop, r, w):
            P.add("pe", lambda e: e.matmul(out, lhsT, rhs, start=start, stop=stop), r, w)

        def TR(out, in_, ident, r, w):
            P.add("pe", lambda e: e.transpose(out, in_, ident), r, w)

        def ACTV(out, in_, func, r, w, bias=None, scale=None):
            kw = {}
            if bias is not None:
                kw["bias"] = bias
            if scale is not None:
                kw["scale"] = scale
            P.add("act", lambda e: e.activation(out, in_, func, **kw), r, w)

        def CP(eng, out, in_, r, w):
            if eng == "act":
                P.add("act", lambda e: e.activation(out, in_, AF.Copy), r, w)
            else:
                P.add(eng, lambda e: e.tensor_copy(out, in_), r, w)

        def TTOP(out, in0, in1, op, r, w, eng="dve"):
            P.add(eng, lambda e: e.tensor_tensor(out, in0, in1, op), r, w)

        def TS(out, in0, s1, op0, r, w, s2=None, op1=None, eng="dve"):
            if op1 is None:
                P.add(eng, lambda e: e.tensor_scalar(out, in0, s1, None, op0), r, w)
            else:
                P.add(eng, lambda e: e.tensor_scalar(out, in0, s1, s2, op0, op1), r, w)

        def STT(out, in0, scalar, in1, op0, op1, r, w, eng="dve"):
            P.add(eng, lambda e: e.scalar_tensor_tensor(out, in0, scalar, in1, op0, op1), r, w)

        def RECIP(out, in_, r, w):
            P.add("dve", lambda e: e.reciprocal(out, in_), r, w)

        def MSET(t, val, w, eng="dve"):
            P.add(eng, lambda e: e.memset(t, val), (), w)

        def DMA(eng, out, in_, r, w, slow=False):
            if slow:
                P.add(eng, lambda e: e.dma_start(out=out, in_=in_, allow_slow_non_contiguous=True), r, w, dma=True)
            else:
                P.add(eng, lambda e: e.dma_start(out=out, in_=in_), r, w, dma=True)

        idf = sb("idf", [128, 128], F32)
        idb = sb("idb", [128, 128], BF16)
        ones_f = sb("ones_f", [128, 128], F32)
        ones_b = sb("ones_b", [128, 128], BF16)
        triu = sb("triu", [128, 128], F32)
        m1 = sb("m1", [128, 128], F32)
        m3 = sb("m3", [128, 128], F32)
        eb = sb("eb", [128, 8, 256], BF16)
        onespad = sb("onespad", [128, 192], BF16)
        stg = sb("stg", [128, 128], F32)
        NPA = 87
        pvA = [sb("pvA%d" % l, [128, NPA], F32) for l in range(depth)]
        pvB = [sb("pvB%d" % l, [128, 66], F32) for l in range(depth)]
        pvF = sb("pvF", [128, 8], F32)
        srow = sb("srow", [1, 16], F32)
        bc = [sb("bc%d" % l, [128, 16], F32) for l in range(depth)]
        expA = [sb("expA%d" % l, [128, 4], F32) for l in range(depth)]
        es8 = [sb("es8%d" % l, [128, 4, 2], F32) for l in range(depth)]
        escol = [sb("escol%d" % l, [128, 4], F32) for l in range(depth)]
        diagw1 = sb("diagw", [128, 12, 4, 128], BF16)
        diagw = [diagw1 for l in range(depth)]
        kdT = [sb("kdT%d" % l, [128, 2, 128 + TT], BF16) for l in range(depth)]
        Vp = [sb("Vp%d" % l, [128, 1 + NB, 2, 192], BF16) for l in range(depth)]
        chalo = [sb("chalo%d" % l, [128, 12, 3], BF16) for l in range(depth)]
        fhalo = [sb("fhalo%d" % l, [128, NFF, 2], BF16) for l in range(depth)]
        S = [sb("S%d" % l, [128, 4, 128], F32) for l in range(depth)]
        Sb = [sb("Sb%d" % l, [128, 4, 128], BF16) for l in range(depth)]
        xT = sb("xT", [128, 8, TT], F32)
        sqb = [sb("sqb%d" % i, [128, TT], BF16) for i in range(2)]
        sqn = [sb("sqn%d" % i, [128, TT], BF16) for i in range(2)]
        rb2 = sb("rb2", [128, TT], F32)
        rstd = sb("rstd", [128, TT], F32)
        hT = sb("hT", [128, 8, TT], BF16)
        qT = sb("qT", [128, 4, TT], BF16)
        cT = sb("cT", [128, 12, 3 + TT], BF16)
        gz = sb("gz", [128, 4, TT], BF16)
        ba = sb("ba", [128, NB, 8], F32)
        wA = [sb("wA%d" % i, [128, 8, 256], BF16) for i in range(4)]
        wk = sb("wk", [128, 8, 256], BF16)
        wv = sb("wv", [128, 8, 136], BF16)
        wD = [sb("wD%d" % i, [128, NFF, 128], BF16) for i in range(2)]
        PT = [sb("PT%d" % i, [128, 256], BF16) for i in range(2)]
        den = sb("den", [128, 128], F32)
        sil = [sb("sil%d" % i, [128, TT], F32) for i in range(2)]
        rb = rstd
        knT = sb("knT", [128, 4, TT], BF16)
        ssk = sb("ssk", [128, NB, 4], F32)
        ktm = sb("ktm", [128, 4, 128], BF16)
        vtm = sb("vtm", [128, 4, 128], BF16)
        col2 = [{n: sb("col%d_" % i + n, [128, 4], F32) for n in
                 ("e1", "spb", "xa", "e2", "spa", "g", "G", "Gl", "Gp", "nG", "L1", "t1", "t2", "kb", "kd", "beta", "glast")}
                for i in range(2)]
        diagG = [sb("diagG%d" % i, [128, 128], F32) for i in range(2)]
        E1 = [sb("E1_%d" % i, [128, 128], F32) for i in range(2)]
        E3 = [sb("E3_%d" % i, [128, 128], F32) for i in range(2)]
        Eq = [sb("Eq_%d" % i, [128, 128], F32) for i in range(2)]
        Xb = [sb("Xb%d" % i, [128, 4, 128], BF16) for i in range(2)]
        XTb = [sb("XTb%d" % i, [128, 4, 128], BF16) for i in range(2)]
        Qb = [sb("Qb%d" % i, [128, 4, 128], BF16) for i in range(2)]
        dbl = lambda n: [sb("%s_%d" % (n, i), [128, 4, 128], BF16) for i in range(2)]
        X0, XT0, Q0 = dbl("X0"), dbl("XT0"), dbl("Q0")
        qkT4, qdT4, vb4, kb4, kd4 = dbl("qkT4"), dbl("qdT4"), dbl("vb4"), dbl("kb4"), dbl("kd4")
        nwT4 = sb("nwT4", [128, 4, 128], BF16)
        vnew4 = sb("vnew4", [128, 4, 128], BF16)
        IpA = dbl("IpA")
        ET4 = sb("ET4", [128, 4, 128], BF16)
        Yv4 = sb("Yv4", [128, 4, 128], BF16)
        Yk4 = sb("Yk4", [128, 4, 128], BF16)
        oT = sb("oT", [128, 4, TT], F32)
        gpre = [sb("gpre%d" % i, [128, 2 + TT], BF16) for i in range(2)]
        diagf = [sb("diagf%d" % i, [128, 3, 128], BF16) for i in range(2)]
        sg = [sb("sg%d" % i, [128, TT], BF16) for i in range(2)]
        big = sb("big", [128, NFF * TT], BF16)
        gT = big[:, :].rearrange("p (j t) -> p j t", j=NFF)
        bigf = big.bitcast(F32)
        xt = bigf[:, 0:NB * D].rearrange("p (t d) -> p t d", t=NB)
        ps = [es.enter_context(nc.psum_tensor("ps%d" % i, [128, 512], F32)) for i in range(8)]
        PSK = ["ps%d" % i for i in range(8)]

        DMA("sp", idf[:], c_ident, (), ["idf"])
        DMA("sp", triu[:], c_triu, (), ["triu"])
        DMA("sp", m1[:], c_m1, (), ["m1"])
        DMA("sp", m3[:], c_m3, (), ["m3"])
        DMA("pool", eb[:], c_eb.rearrange("p (h c) -> p h c", h=8), (), ["eb"])
        for l in range(depth):
            for (src, dst, nm, rows, rb_) in ((w_in_d[l], wsc_in[l], "wsc_in%d" % l, D, 256), (w_out_d[l], wsc_out[l], "wsc_out%d" % l, D, 512),
                                              (w_ffn_d[l], wsc_ffn[l], "wsc_ffn%d" % l, D, 128), (w_down_d[l], wsc_dn[l], "wsc_dn%d" % l, D_FF, 256)):
                for i, r0 in enumerate(range(0, rows, rb_)):
                    DMA("pool", dst[r0:r0 + rb_, :], src[r0:r0 + rb_, :], (), [(nm, i)])
        MSET(ones_f[:], 1.0, ["ones_f"])
        MSET(ones_b[:], 1.0, ["ones_b"])
        MSET(onespad[:], 0.0, ["onespad"])
        MSET(onespad[:, 64:128], 1.0, ["onespad"])
        CP("act", idb[:], idf[:], ["idf"], ["idb"])
        for l in range(depth):
            MSET(Vp[l][:], 0.0, ["Vp%d" % l])
            MSET(kdT[l][:], 0.0, ["kdT%d" % l])
            MSET(chalo[l][:], 0.0, ["chalo%d" % l])
            MSET(fhalo[l][:], 0.0, ["fhalo%d" % l])
            MSET(S[l][:], 0.0, ["S%d" % l])
            MSET(Sb[l][:], 0.0, ["Sb%d" % l])
        for l in range(depth):
            DMA("sp", stg[0:8, :], attn_norm_d[l].rearrange("(c p) -> c p", p=128), (), [("stg", 0)])
            DMA("sp", stg[8:16, :], ffn_norm_d[l].rearrange("(c p) -> c p", p=128), (), [("stg", 1)])
            DMA("sp", stg[16:38, :], fbias_d[l].rearrange("(c p) -> c p", p=128), (), [("stg", 2)])
            DMA("sp", stg[38:39, :], gnorm_d[l].rearrange("(c p) -> c p", p=128), (), [("stg", 3)])
            DMA("sp", stg[39:87, :], gconv_d[l].rearrange("t (j p) -> (t j) p", p=128), (), [("stg", 4)])
            TR(ps[0][:, 0:NPA], stg[0:NPA, :], idf[0:NPA, 0:NPA], ["stg", "idf"], [PSK[0]])
            CP("dve", pvA[l][:], ps[0][:, 0:NPA], [PSK[0]], ["pvA%d" % l])
            DMA("sp", stg[0:66, :], fconv_d[l].rearrange("t (j p) -> (t j) p", p=128), (), ["stg"])
            TR(ps[1][:, 0:66], stg[0:66, :], idf[0:66, 0:66], ["stg", "idf"], [PSK[1]])
            CP("dve", pvB[l][:], ps[1][:, 0:66], [PSK[1]], ["pvB%d" % l])
            DMA("sp", srow[0:1, 0:4], dtb_d[l].rearrange("(o c) -> o c", o=1), (), [("srow", 0)])
            DMA("sp", srow[0:1, 4:8], alog_d[l].rearrange("(o c) -> o c", o=1), (), [("srow", 1)])
            DMA("sp", srow[0:1, 8:16], sinks_d[l].rearrange("(o c) -> o c", o=1), (), [("srow", 2)])
            MM(ps[2][:, 0:16], ones_f[0:1, :], srow[0:1, :], True, True, ["ones_f", "srow"], [PSK[2]])
            CP("dve", bc[l][:], ps[2][:, 0:16], [PSK[2]], ["bc%d" % l])
            ACTV(expA[l][:], bc[l][:, 4:8], AF.Exp, ["bc%d" % l], ["expA%d" % l])
            ACTV(es8[l][:], bc[l][:, 8:16].rearrange("p (c two) -> p c two", two=2), AF.Exp, ["bc%d" % l], ["es8%d" % l])
            CP("dve", escol[l][0:64, :], es8[l][0:64, :, 0], ["es8%d" % l], [("escol%d" % l, 0)])
            CP("dve", escol[l][64:128, :], es8[l][64:128, :, 1], ["es8%d" % l], [("escol%d" % l, 1)])
        DMA("sp", stg[0:8, :], fnorm_d.rearrange("(c p) -> c p", p=128), (), ["stg"])
        TR(ps[0][:, 0:8], stg[0:8, :], idf[0:8, 0:8], ["stg", "idf"], [PSK[0]])
        CP("dve", pvF[:], ps[0][:, 0:8], [PSK[0]], ["pvF"])

        wstate = {"i": 0}

        def next_wA():
            i = wstate["i"] % len(wA)
            wstate["i"] += 1
            return wA[i], "wA%d" % i

        pjs = {"i": 0}

        def next_pj():
            i = pjs["i"] % 2
            pjs["i"] += 1
            return ps[i], PSK[i]

        def norm_accum(c):
            s = sqn[c % 2]
            ACTV(s[:], xT[:, c, :], AF.Square, [("xT", c)], ["sqn%d" % (c % 2)])
            MM(ps[6][:], ones_b[:], s[:], c == 0, c == 7, ["ones_b", "sqn%d" % (c % 2)], [PSK[6]])

        def norm_finish_rstd():
            ACTV(rstd[:], ps[6][:], AF.Sqrt, [PSK[6]], ["rstd"], bias=EPS, scale=1.0 / D)
            RECIP(rstd[:], rstd[:], ["rstd"], ["rstd"])

        def rmsnorm_to_hT(wcol):
            norm_finish_rstd()
            for c in range(8):
                STT(hT[:, c, :], xT[:, c, :], wcol(c), rstd[:], ALU.mult, ALU.mult, [("xT", c), "rstd"], [("hT", c)])

        def proj_fm(wt, wkey, col0, nchunks, evac):
            for f in range(nchunks):
                pt, pk = next_pj()
                for k in range(8):
                    MM(pt[:], wt[:, k, col0 + f * 128: col0 + (f + 1) * 128], hT[:, k, :], k == 0, k == 7,
                       [wkey, ("hT", k)], [pk])
                evac(f, pt, pk)

        for it in range(ntiles):
            first = (it == 0)
            DMA("sp", xt, x_d[it * TT:(it + 1) * TT, :].rearrange("(t p) d -> p t d", p=128), (), ["big"])
            for c in range(8):
                pt, pk = next_pj()
                for t in range(NB):
                    TR(pt[:, t * 128:(t + 1) * 128], xt[:, t, c * 128:(c + 1) * 128], idf[:], ["big", "idf"], [pk])
                CP("dve", xT[:, c, :], pt[:], [pk], [("xT", c)])
                norm_accum(c)

            for l in range(depth):
                L = "%d" % l
                w_in3 = wsc_in[l].rearrange("(k p) n -> p k n", p=128)
                WIN = "wsc_in" + L
                rmsnorm_to_hT(lambda c: pvA[l][:, c:c + 1])
                for j in range(12):
                    for t in range(4):
                        TS(diagw1[:, j, t, :], idb[:], pvA[l][:, 39 + t * 12 + j: 40 + t * 12 + j], ALU.mult,
                           ["idb", "pvA" + L], [("diagw", j, t)])
                for g in range(2):
                    wt, wkey = next_wA()
                    DMA("sp", wt[:], w_in3[:, :, g * 256:(g + 1) * 256], [WIN], [wkey])
                    proj_fm(wt, wkey, 0, 2, lambda f, pt, pk, g=g: CP("act", qT[:, g * 2 + f, :], pt[:], [pk], [("qT", g * 2 + f)]))
                for j in range(2):
                    for h in range(2):
                        DMA("sp", wk[:, :, j * 128 + h * 64: j * 128 + h * 64 + 64], w_in3[:, :, 512 + j * 64: 576 + j * 64],
                            [WIN], [("wk", j, h)])
                proj_fm(wk, "wk", 0, 2,
                        lambda f, pt, pk: CP("dve", kdT[l][:, f, 128:128 + TT], pt[:], [pk], [("kdT" + L, f, 1)]))
                DMA("sp", wv[:, :, 0:128], w_in3[:, :, 640:768], [WIN], [("wv", 0)])
                DMA("sp", wv[:, :, 128:136], w_in3[:, :, 2816:2824], [WIN], [("wv", 1)])
                for t in range(NB):
                    pt, pk = next_pj()
                    for k in range(8):
                        MM(pt[:, 0:136], hT[:, k, t * 128:(t + 1) * 128], wv[:, k, :], k == 0, k == 7, ["wv", ("hT", k)], [pk])
                    CP("act", Vp[l][:, 1 + t, :, 64:128], pt[:, 0:128].rearrange("p (k d) -> p k d", k=2), [pk], [("Vp" + L, 1 + t)])
                    CP("dve", ba[:, t, :], pt[:, 128:136], [pk], [("ba", t)])
                CP("dve", cT[:, :, 0:3], chalo[l][:], ["chalo" + L], [("cT", "halo")])

                def gdn_proj(g6):
                    wt, wkey = next_wA()
                    DMA("sp", wt[:], w_in3[:, :, 768 + g6 * 256: 768 + (g6 + 1) * 256], [WIN], [wkey])
                    proj_fm(wt, wkey, 0, 2,
                            lambda f, pt, pk, g6=g6: CP("act" if f % 2 else "dve", cT[:, g6 * 2 + f, 3:3 + TT], pt[:], [pk],
                                                        [("cT", g6 * 2 + f)]))

                def fm_conv(idx):
                    hh = idx % 4
                    isk = idx >= 4
                    j = idx
                    pt, pk = (ps[5], PSK[5]) if idx % 2 == 0 else (ps[7], PSK[7])
                    for tap in range(4):
                        MM(pt[:], diagw1[:, j, tap, :], cT[:, j, tap:tap + TT], tap == 0, tap == 3,
                           [("diagw", j), ("cT", j), ("cT", "halo")], [pk])
                    sl = sil[idx % 2]
                    slk = "sil%d" % (idx % 2)
                    ACTV(sl[:], pt[:], AF.Silu, [pk], [slk])
                    sq_ = sqb[idx % 2]
                    sqk = "sqb%d" % (idx % 2)
                    ACTV(sq_[:], sl[:], AF.Square, [slk], [sqk])
                    ssb, ssbk = (ps[2], PSK[2]) if idx % 2 == 0 else (ps[4], PSK[4])
                    rbx, rbxk = (rstd, "rstd") if idx % 2 == 0 else (rb2, "rb2")
                    MM(ssb[:], ones_b[:], sq_[:], True, True, ["ones_b", sqk], [ssbk])
                    if isk:
                        for tb in range(NB):
                            MM(ps[3][:, tb * 4 + hh: tb * 4 + hh + 1], sq_[:, tb * 128:(tb + 1) * 128], ones_b[:, 0:1],
                               (hh == 0 and tb == 0), (hh == 3 and tb == NB - 1), [sqk, "ones_b"], [PSK[3]])
                    ACTV(rbx[:], ssb[:], AF.Sqrt, [ssbk], [rbxk], bias=EPS, scale=1.0)
                    RECIP(rbx[:], rbx[:], [rbxk], [rbxk])
                    if isk:
                        TTOP(knT[:, hh, :], sl[:], rbx[:], ALU.mult, [slk, rbxk], [("knT", hh)])
                    else:
                        TTOP(hT[:, 4 + hh, :], sl[:], rbx[:], ALU.mult, [slk, rbxk], [("hT", 4 + hh)])

                for g6 in range(4):
                    gdn_proj(g6)

                def rest_of_inproj():
                    for g6 in range(4, 6):
                        gdn_proj(g6)
                    CP("dve", chalo[l][:], cT[:, :, TT:TT + 3], ["cT"], ["chalo" + L])
                    for g in range(2):
                        wt, wkey = next_wA()
                        DMA("sp", wt[:], w_in3[:, :, 2304 + g * 256: 2304 + (g + 1) * 256], [WIN], [wkey])
                        proj_fm(wt, wkey, 0, 2, lambda f, pt, pk, g=g: ACTV(gz[:, g * 2 + f, :], pt[:], AF.Silu, [pk], [("gz", g * 2 + f)]))

                lane_p = P.capture(rest_of_inproj)
                lane_c = P.capture(lambda: [fm_conv(i) for i in range(4, 8)])
                P.replay(merge_lanes([lane_p, lane_c]))

                for idx in range(4):
                    fm_conv(idx)
                CP("dve", ssk[:], ps[3][:, 0:NB * 4].rearrange("p (t h) -> p t h", h=4), [PSK[3]], ["ssk"])

                def attention_lane():
                    for tb in range(NB):
                        noprev = first and tb == 0
                        qs = slice(tb * 128, (tb + 1) * 128)
                        kprev = slice(tb * 128, (tb + 1) * 128)
                        kcur = slice(128 + tb * 128, 128 + (tb + 1) * 128)
                        for c in range(4):
                            j = c // 2
                            for par in range(2):
                                rows = slice(par * 64, par * 64 + 64)
                                sbk, skey = ps[2 + par], PSK[2 + par]
                                rk = [("kdT" + L, j), ("qT", c)]
                                if not noprev:
                                    MM(sbk[:, 0:128], kdT[l][rows, j, kprev], qT[rows, c, qs], True, False, rk, [skey])
                                    MM(sbk[:, 128:256], kdT[l][rows, j, kcur], qT[rows, c, qs], False, True, rk, [skey])
                                    ACTV(PT[par][:], sbk[:, 0:256], AF.Exp, [skey], ["PT%d" % par], scale=0.125)
                                    TTOP(PT[par][:], PT[par][:], eb[:, 2 * c + par, :], ALU.mult, ["PT%d" % par, "eb"], ["PT%d" % par])
                                else:
                                    MM(sbk[:, 128:256], kdT[l][rows, j, kcur], qT[rows, c, qs], True, True, rk, [skey])
                                    ACTV(PT[par][:, 128:256], sbk[:, 128:256], AF.Exp, [skey], ["PT%d" % par], scale=0.125)
                                    TTOP(PT[par][:, 128:256], PT[par][:, 128:256], eb[:, 2 * c + par, 128:256], ALU.mult,
                                         ["PT%d" % par, "eb"], ["PT%d" % par])
                            ob, okey = ps[4], PSK[4]
                            mms = []
                            for par in range(2):
                                vs = slice(64, 192) if par == 0 else slice(0, 128)
                                if not noprev:
                                    mms.append((0, Vp[l][:, tb, j, vs], PT[par][:, 0:128], par, ("Vp" + L, tb)))
                                mms.append((0, Vp[l][:, 1 + tb, j, vs], PT[par][:, 128:256], par, ("Vp" + L, 1 + tb)))
                            for par in range(2):
                                vs = slice(64, 192) if par == 0 else slice(0, 128)
                                if not noprev:
                                    mms.append((1, onespad[:, vs], PT[par][:, 0:128], par, "onespad"))
                                mms.append((1, onespad[:, vs], PT[par][:, 128:256], par, "onespad"))
                            for n, (reg, lt, rh, par, lkey) in enumerate(mms):
                                MM(ob[:, reg * 128:(reg + 1) * 128], lt, rh, n == 0, n == len(mms) - 1, [lkey, "PT%d" % par], [okey])
                            TS(den[:], ob[:, 128:256], escol[l][:, c:c + 1], ALU.add, [okey, "escol" + L], ["den"])
                            RECIP(den[:], den[:], ["den"], ["den"])
                            TTOP(hT[:, c, qs], ob[:, 0:128], den[:], ALU.mult, [okey, "den"], [("hT", c)])
                    CP("dve", kdT[l][:, :, 0:128], kdT[l][:, :, TT:TT + 128], ["kdT" + L], ["kdT" + L])
                    CP("dve", Vp[l][:, 0, :, :], Vp[l][:, NB, :, :], ["Vp" + L], ["Vp" + L])

                def stage_a(tb):
                    bp = tb % 2
                    B_ = "_%d" % bp
                    blk = slice(tb * 128, (tb + 1) * 128)
                    C = col2[bp]
                    ck = lambda n: "c%d_%s" % (bp, n)
                    for (base, dstt, dkey, bank) in ((4, ktm, "ktm", 0), (8, vtm, "vtm", 1)):
                        for hh in range(4):
                            for tap in range(4):
                                MM(ps[bank][:, hh * 128:(hh + 1) * 128], cT[:, base + hh, tb * 128 + tap: tb * 128 + tap + 128],
                                   diagw1[:, base + hh, tap, :], (hh == 0 and tap == 0), (hh == 3 and tap == 3),
                                   [("cT", base + hh), ("cT", "halo"), ("diagw", base + hh)], [PSK[bank]])
                        ACTV(dstt[:], ps[bank][:].rearrange("p (h d) -> p h d", h=4), AF.Silu, [PSK[bank]], [dkey])
                    ACTV(C["e1"][:], ba[:, tb, 0:4], AF.Exp, [("ba", tb)], [ck("e1")], scale=-1.0)
                    ACTV(C["spb"][:], C["e1"][:], AF.Ln, [ck("e1")], [ck("spb")], bias=1.0)
                    TTOP(C["xa"][:], ba[:, tb, 4:8], bc[l][:, 0:4], ALU.add, [("ba", tb), "bc" + L], [ck("xa")])
                    ACTV(C["e2"][:], C["xa"][:], AF.Exp, [ck("xa")], [ck("e2")])
                    ACTV(C["spa"][:], C["e2"][:], AF.Ln, [ck("e2")], [ck("spa")], bias=1.0)
                    STT(C["g"][:], C["spa"][:], -1.0, expA[l][:], ALU.mult, ALU.mult, [ck("spa"), "expA" + L], [ck("g")])
                    MM(ps[0][:, 0:4], triu[:], C["g"][:], True, False, ["triu", ck("g")], [PSK[0]])
                    MM(ps[0][:, 4:8], ones_f[:], C["g"][:], False, True, ["ones_f", ck("g")], [PSK[0]])
                    CP("dve", C["G"][:], ps[0][:, 0:4], [PSK[0]], [ck("G")])
                    CP("dve", C["Gl"][:], ps[0][:, 4:8], [PSK[0]], [ck("Gl")])
                    TTOP(C["Gp"][:], C["G"][:], C["spb"][:], ALU.subtract, [ck("G"), ck("spb")], [ck("Gp")])
                    TS(C["nG"][:], C["G"][:], -1.0, ALU.mult, [ck("G")], [ck("nG")])
                    ACTV(C["L1"][:], ssk[:, tb, :], AF.Ln, ["ssk"], [ck("L1")], bias=EPS)
                    STT(C["t1"][:], C["L1"][:], -0.5, C["Gp"][:], ALU.mult, ALU.add, [ck("L1"), ck("Gp")], [ck("t1")])
                    ACTV(C["kb"][:], C["t1"][:], AF.Exp, [ck("t1")], [ck("kb")])
                    TTOP(C["t2"][:], C["Gl"][:], C["G"][:], ALU.subtract, [ck("Gl"), ck("G")], [ck("t2")])
                    STT(C["t2"][:], C["L1"][:], -0.5, C["t2"][:], ALU.mult, ALU.add, [ck("L1"), ck("t2")], [ck("t2")])
                    ACTV(C["kd"][:], C["t2"][:], AF.Exp, [ck("t2")], [ck("kd")])
                    ACTV(C["beta"][:], C["spb"][:], AF.Exp, [ck("spb")], [ck("beta")], scale=-1.0)
                    ACTV(C["glast"][:], C["Gl"][:], AF.Exp, [ck("Gl")], [ck("glast")])
                    for hh in range(4):
                        TS(vb4[bp][:, hh, :], vtm[:, hh, :], C["beta"][:, hh:hh + 1], ALU.mult, ["vtm", ck("beta")], [("vb4" + B_, hh)])
                        TS(kb4[bp][:, hh, :], ktm[:, hh, :], C["kb"][:, hh:hh + 1], ALU.mult, ["ktm", ck("kb")], [("kb4" + B_, hh)])
                        TS(kd4[bp][:, hh, :], ktm[:, hh, :], C["kd"][:, hh:hh + 1], ALU.mult, ["ktm", ck("kd")], [("kd4" + B_, hh)])
                    for hh in range(4):
                        pr = hh % 2
                        rbk, rkey = ps[0], PSK[0]
                        kbk, kkey = ps[1], PSK[1]
                        dg, dgk = diagG[pr], "diagG%d" % pr
                        TS(dg[:], idf[:], C["G"][:, hh:hh + 1], ALU.mult, ["idf", ck("G")], [dgk])
                        MM(rbk[:, 0:128], ones_f[:], dg[:], True, False, ["ones_f", dgk], [rkey])
                        MM(rbk[:, 0:128], idf[:], m1[:], False, False, ["idf", "m1"], [rkey])
                        MM(rbk[:, 128:256], ones_f[:], dg[:], False, False, ["ones_f", dgk], [rkey])
                        MM(rbk[:, 128:256], idf[:], m3[:], False, False, ["idf", "m3"], [rkey])
                        MM(rbk[:, 256:384], ones_f[:], dg[:], False, True, ["ones_f", dgk], [rkey])
                        ACTV(E1[pr][:], rbk[:, 0:128], AF.Exp, [rkey, ck("Gp")], ["E1_%d" % pr], bias=C["Gp"][:, hh:hh + 1], scale=-1.0)
                        ACTV(E3[pr][:], rbk[:, 128:256], AF.Exp, [rkey, ck("nG")], ["E3_%d" % pr], bias=C["nG"][:, hh:hh + 1], scale=1.0)
                        ACTV(Eq[pr][:], rbk[:, 256:384], AF.Exp, [rkey], ["Eq_%d" % pr])
                        MM(kbk[:, 0:128], knT[:, hh, blk], knT[:, hh, blk], True, False, [("knT", hh)], [kkey])
                        MM(kbk[:, 128:256], knT[:, hh, blk], hT[:, 4 + hh, blk], False, True, [("knT", hh), ("hT", 4 + hh)], [kkey])
                        STT(X0[bp][:, hh, :], kbk[:, 0:128], -1.0, E1[pr][:], ALU.mult, ALU.mult, [kkey, "E1_%d" % pr], [("X0" + B_, hh)])
                        TTOP(IpA[bp][:, hh, :], idb[:], X0[bp][:, hh, :], ALU.subtract, ["idb", ("X0" + B_, hh)], [("IpA" + B_, hh)])
                        STT(qkT4[bp][:, hh, :], kbk[:, 128:256], 128.0 ** -0.5, E3[pr][:], ALU.mult, ALU.mult, [kkey, "E3_%d" % pr], [("qkT4" + B_, hh)])
                        STT(qdT4[bp][:, hh, :], hT[:, 4 + hh, blk], 128.0 ** -0.5, Eq[pr][:], ALU.mult, ALU.mult,
                            [("hT", 4 + hh), "Eq_%d" % pr], [("qdT4" + B_, hh)])
                    for hh in range(4):
                        MM(ps[1][:, hh * 128:(hh + 1) * 128], X0[bp][:, hh, :], idb[:], hh == 0, hh == 3, [("X0" + B_, hh), "idb"], [PSK[1]])
                    CP("act", XT0[bp][:], ps[1][:].rearrange("p (h d) -> p h d", h=4), [PSK[1]], ["XT0" + B_])
                    for hh in range(4):
                        TTOP(Q0[bp][:, hh, :], ps[1][:, hh * 128:(hh + 1) * 128], idf[:], ALU.add, [PSK[1], "idf"], [("Q0" + B_, hh)])

                def stage_b(tb):
                    bp = tb % 2
                    B_ = "_%d" % bp
                    blk = slice(tb * 128, (tb + 1) * 128)
                    C = col2[bp]
                    NLEV = 6
                    for lev in range(1, NLEV + 1):
                        p1 = lev % 2
                        if lev == 1:
                            Xs, XTs, Qs = X0[bp], XT0[bp], Q0[bp]
                            Xk, XTk, Qk = "X0" + B_, "XT0" + B_, "Q0" + B_
                        else:
                            p0 = (lev - 1) % 2
                            Xs, XTs, Qs = Xb[p0], XTb[p0], Qb[p0]
                            Xk, XTk, Qk = "Xb%d" % p0, "XTb%d" % p0, "Qb%d" % p0
                        for hh in range(4):
                            MM(ps[5][:, hh * 128:(hh + 1) * 128], XTs[:, hh, :], Xs[:, hh, :], hh == 0, hh == 3, [XTk, Xk], [PSK[5]])
                        if lev < NLEV:
                            for hh in range(4):
                                MM(ps[6][:, hh * 128:(hh + 1) * 128], Xs[:, hh, :], XTs[:, hh, :], hh == 0, hh == 3, [XTk, Xk], [PSK[6]])
                        CP("act", Xb[p1][:], ps[5][:].rearrange("p (h d) -> p h d", h=4), [PSK[5]], ["Xb%d" % p1])
                        if lev < NLEV:
                            CP("dve", XTb[p1][:], ps[6][:].rearrange("p (h d) -> p h d", h=4), [PSK[6]], ["XTb%d" % p1])
                        for hh in range(4):
                            MM(ps[7][:, hh * 128:(hh + 1) * 128], Xb[p1][:, hh, :], Qs[:, hh, :], hh == 0, hh == 3, ["Xb%d" % p1, Qk], [PSK[7]])
                        TTOP(Qb[p1][:], ps[7][:].rearrange("p (h d) -> p h d", h=4), Qs[:], ALU.add, [PSK[7], Qk], ["Qb%d" % p1])
                    TTt = Qb[NLEV % 2]
                    TTk = "Qb%d" % (NLEV % 2)
                    for hh in range(4):
                        MM(ps[5][:, hh * 128:(hh + 1) * 128], IpA[bp][:, hh, :], TTt[:, hh, :], hh == 0, hh == 3, [("IpA" + B_, hh), TTk], [PSK[5]])
                    for hh in range(4):
                        STT(ET4[:, hh, :], ps[5][:, hh * 128:(hh + 1) * 128], -1.0, idf[:], ALU.mult, ALU.add, [PSK[5], "idf"], [("ET4", hh)])
                    for hh in range(4):
                        MM(ps[6][:, hh * 128:(hh + 1) * 128], TTt[:, hh, :], vb4[bp][:, hh, :], hh == 0, hh == 3, [TTk, ("vb4" + B_, hh)], [PSK[6]])
                    for hh in range(4):
                        MM(ps[7][:, hh * 128:(hh + 1) * 128], TTt[:, hh, :], kb4[bp][:, hh, :], hh == 0, hh == 3, [TTk, ("kb4" + B_, hh)], [PSK[7]])
                    CP("act", Yv4[:], ps[6][:].rearrange("p (h d) -> p h d", h=4), [PSK[6]], ["Yv4"])
                    CP("dve", Yk4[:], ps[7][:].rearrange("p (h d) -> p h d", h=4), [PSK[7]], ["Yk4"])
                    for hh in range(4):
                        MM(ps[5][:, hh * 128:(hh + 1) * 128], kb4[bp][:, hh, :], TTt[:, hh, :], hh == 0, False, [("kb4" + B_, hh), TTk], [PSK[5]])
                        MM(ps[5][:, hh * 128:(hh + 1) * 128], Yk4[:, hh, :], ET4[:, hh, :], False, hh == 3, ["Yk4", ("ET4", hh)], [PSK[5]])
                    ACTV(nwT4[:], ps[5][:].rearrange("p (h d) -> p h d", h=4), AF.Copy, [PSK[5]], ["nwT4"], scale=-1.0)
                    for hh in range(4):
                        MM(ps[6][:, hh * 128:(hh + 1) * 128], TTt[:, hh, :], vb4[bp][:, hh, :], hh == 0, False, [TTk, ("vb4" + B_, hh)], [PSK[6]])
                        MM(ps[6][:, hh * 128:(hh + 1) * 128], ET4[:, hh, :], Yv4[:, hh, :], False, False, [("ET4", hh), "Yv4"], [PSK[6]])
                        MM(ps[6][:, hh * 128:(hh + 1) * 128], nwT4[:, hh, :], Sb[l][:, hh, :], False, hh == 3, ["nwT4", "Sb" + L], [PSK[6]])
                    CP("dve", vnew4[:], ps[6][:].rearrange("p (h d) -> p h d", h=4), [PSK[6]], ["vnew4"])
                    for hh in range(4):
                        MM(ps[7][:, hh * 128:(hh + 1) * 128], Sb[l][:, hh, :], qdT4[bp][:, hh, :], hh == 0, False, ["Sb" + L, ("qdT4" + B_, hh)], [PSK[7]])
                        MM(ps[7][:, hh * 128:(hh + 1) * 128], vnew4[:, hh, :], qkT4[bp][:, hh, :], False, hh == 3, ["vnew4", ("qkT4" + B_, hh)], [PSK[7]])
                    CP("act", oT[:, :, blk], ps[7][:].rearrange("p (h d) -> p h d", h=4), [PSK[7]], [("oT", tb)])
                    for hh in range(4):
                        MM(ps[5][:, hh * 128:(hh + 1) * 128], kd4[bp][:, hh, :], vnew4[:, hh, :], hh == 0, hh == 3, [("kd4" + B_, hh), "vnew4"], [PSK[5]])
                    for hh in range(4):
                        STT(S[l][:, hh, :], S[l][:, hh, :], C["glast"][:, hh:hh + 1], ps[5][:, hh * 128:(hh + 1) * 128], ALU.mult, ALU.add,
                            ["S" + L, "c%d_glast" % bp, PSK[5]], [("S" + L, hh)])
                    CP("act", Sb[l][:], S[l][:], ["S" + L], ["Sb" + L])

                lane_attn = P.capture(attention_lane)
                lanes_a = [P.capture(lambda tb=tb: stage_a(tb)) for tb in range(NB)]
                lanes_b = [P.capture(lambda tb=tb: stage_b(tb)) for tb in range(NB)]
                seq = list(lanes_a[0])
                for k in range(NB):
                    seq += merge_lanes([lanes_b[k]] + ([lanes_a[k + 1]] if k + 1 < NB else []))
                P.replay(merge_lanes([seq, lane_attn]))
                if dbg and it == 0 and l == 0:
                    DMA("sp", dbg_d["oT"].rearrange("(c p) t -> p c t", p=128), oT[:], ["oT"], ())
                for hh in range(4):
                    s = sqb[hh % 2]
                    sk = "sqb%d" % (hh % 2)
                    ACTV(s[:], oT[:, hh, :], AF.Square, ["oT"], [sk])
                    gsb, gsbk = (ps[2], PSK[2]) if hh % 2 == 0 else (ps[4], PSK[4])
                    grb, grbk = (rstd, "rstd") if hh % 2 == 0 else (rb2, "rb2")
                    MM(gsb[:], ones_b[:], s[:], True, True, ["ones_b", sk], [gsbk])
                    ACTV(grb[:], gsb[:], AF.Sqrt, [gsbk], [grbk], bias=EPS, scale=1.0 / 128)
                    RECIP(grb[:], grb[:], [grbk], [grbk])
                    sl = sil[hh % 2]
                    slk = "sil%d" % (hh % 2)
                    STT(sl[:], oT[:, hh, :], pvA[l][:, 38:39], grb[:], ALU.mult, ALU.mult, ["oT", grbk, "pvA" + L], [slk])
                    TTOP(hT[:, 4 + hh, :], sl[:], gz[:, hh, :], ALU.mult, [slk, ("gz", hh)], [("hT", 4 + hh)])
                if dbg and it == 0 and l == 0:
                    for c in range(8):
                        CP("act", sil[c % 2][:], hT[:, c, :], [("hT", c)], ["sil%d" % (c % 2)])
                        DMA("sp", dbg_d["mixed"][c * 128:(c + 1) * 128, :], sil[c % 2][:], ["sil%d" % (c % 2)], ())

                w_out3 = wsc_out[l].rearrange("(k p) n -> p k n", p=128)
                for grp in range(4):
                    wt, wkey = next_wA()
                    DMA("sp", wt[:], w_out3[:, :, grp * 256:(grp + 1) * 256], ["wsc_out" + L], [wkey])
                    proj_fm(wt, wkey, 0, 2,
                            lambda f, pt, pk, grp=grp: (TTOP(xT[:, grp * 2 + f, :], xT[:, grp * 2 + f, :], pt[:], ALU.add,
                                                             [pk, ("xT", grp * 2 + f)], [("xT", grp * 2 + f)]),
                                                        norm_accum(grp * 2 + f)))
                if dbg and it == 0 and l == 0:
                    DMA("sp", dbg_d["xmid"].rearrange("(c p) t -> p c t", p=128), xT[:], ["xT"], ())

                rmsnorm_to_hT(lambda c: pvA[l][:, 8 + c:9 + c])
                w_ffn3 = wsc_ffn[l].rearrange("(k p) n -> p k n", p=128)
                wtiles = {}

                def ffn_front(j):
                    if j % 2 == 0:
                        nj = min(2, NFF - j)
                        wg, wgk = next_wA()
                        DMA("sp", wg[:, :, 0:nj * 128], w_ffn3[:, :, j * 128:(j + nj) * 128], ["wsc_ffn" + L], [wgk])
                        wu, wuk = next_wA()
                        DMA("sp", wu[:, :, 0:nj * 128], w_ffn3[:, :, D_FF + j * 128: D_FF + (j + nj) * 128], ["wsc_ffn" + L], [wuk])
                        wtiles["cur"] = (wg, wgk, wu, wuk)
                    wg, wgk, wu, wuk = wtiles["cur"]
                    jj = j % 2
                    r2 = j % 2
                    gp, gpk = gpre[r2], "gpre%d" % r2
                    df, dfk = diagf[r2], "diagf%d" % r2
                    CP("dve", gp[:, 0:2], fhalo[l][:, j, :], [("fhalo" + L, j)], [(gpk, "h")])
                    pt, pk = ps[r2], PSK[r2]
                    for k in range(8):
                        MM(pt[:], wg[:, k, jj * 128:(jj + 1) * 128], hT[:, k, :], k == 0, k == 7, [wgk, ("hT", k)], [pk])
                    CP("act", gp[:, 2:2 + TT], pt[:], [pk], [(gpk, "b")])
                    CP("dve", fhalo[l][:, j, :], gp[:, TT:TT + 2], [(gpk, "b")], [("fhalo" + L, j)])
                    for tap in range(3):
                        TS(df[:, tap, :], idb[:], pvB[l][:, tap * NFF + j: tap * NFF + j + 1], ALU.mult, ["idb", "pvB" + L], [(dfk, tap)])
                    pu, puk = ps[3 + r2], PSK[3 + r2]
                    for k in range(8):
                        MM(pu[:], wu[:, k, jj * 128:(jj + 1) * 128], hT[:, k, :], k == 0, k == 7, [wuk, ("hT", k)], [puk])

                def ffn_back(j):
                    r2 = j % 2
                    gp, gpk = gpre[r2], "gpre%d" % r2
                    df, dfk = diagf[r2], "diagf%d" % r2
                    pc, pck = (ps[2], PSK[2]) if r2 == 0 else (ps[5], PSK[5])
                    pu, puk = ps[3 + r2], PSK[3 + r2]
                    for tap in range(3):
                        MM(pc[:], df[:, tap, :], gp[:, tap:tap + TT], tap == 0, tap == 2, [dfk, gpk], [pck])
                    ACTV(sg[r2][:], pc[:], AF.Silu, [pck, "pvA" + L], ["sg%d" % r2], bias=pvA[l][:, 16 + j:17 + j])
                    TTOP(gT[:, j, :], pu[:], sg[r2][:], ALU.mult, [puk, "sg%d" % r2], [("big", j)])

                for j in range(NFF + 1):
                    if j < NFF:
                        ffn_front(j)
                    if j >= 1:
                        ffn_back(j - 1)
                w_dn3 = wsc_dn[l].rearrange("(k p) n -> p k n", p=128)
                for grp in range(8):
                    wd_, wdk = wD[grp % 2], "wD%d" % (grp % 2)
                    DMA("sp", wd_[:], w_dn3[:, :, grp * 128:(grp + 1) * 128], ["wsc_dn" + L], [wdk])
                    for mmi in range(1):
                        m = grp
                        pt, pk = next_pj()
                        for j in range(NFF):
                            MM(pt[:], wd_[:, j, mmi * 128:(mmi + 1) * 128], gT[:, j, :], j == 0, j == NFF - 1, [wdk, ("big", j)], [pk])
                        TTOP(xT[:, m, :], xT[:, m, :], pt[:], ALU.add, [pk, ("xT", m)], [("xT", m)])
                        norm_accum(m)
                if dbg and it == 0 and l == 0:
                    DMA("sp", dbg_d["xout"].rearrange("(c p) t -> p c t", p=128), xT[:], ["xT"], ())

            norm_finish_rstd()
            for c in range(8):
                hf, hfk = sil[c % 2], "sil%d" % (c % 2)
                STT(hf[:], xT[:, c, :], pvF[:, c:c + 1], rstd[:], ALU.mult, ALU.mult, [("xT", c), "rstd", "pvF"], [hfk])
                pt, pk = next_pj()
                for t in range(NB):
                    TR(pt[:, t * 128:(t + 1) * 128], hf[:, t * 128:(t + 1) * 128], idf[:], [hfk, "idf"], [pk])
                CP("act" if c % 2 else "dve", xt[:, :, c * 128:(c + 1) * 128], pt[:].rearrange("p (t d) -> p t d", t=NB), [pk], ["big"])
            DMA("sp", out_d[it * TT:(it + 1) * TT, :].rearrange("(t p) d -> p t d", p=128), xt, ["big"], ())
        P.emit()
    return nc


_PARAM_NAMES = ("attn_norm", "w_in", "attn_sinks", "gdn_conv_w", "gdn_a_log", "gdn_dt_bias", "gdn_norm", "w_out",
                "ffn_norm", "w_ffn_in", "ffn_conv_w", "ffn_conv_b", "w_down", "final_norm")


def kernel(**inputs):
    x = np.ascontiguousarray(np.asarray(inputs["x"], dtype=np.float32))
    B = x.shape[0]
    params = {n: np.ascontiguousarray(np.asarray(inputs[n], dtype=np.float32)) for n in _PARAM_NAMES}
    consts = host_consts()
    nc = build_program(SEQ // TT, DEPTH)
    in_maps = []
    for b in range(B):
        m = {"x": x[b]}
        m.update(params)
        m.update(consts)
        in_maps.append(m)
    res = run_bass_kernel_spmd(nc, in_maps, core_ids=list(range(B)))
    return np.stack([np.asarray(r["out"], dtype=np.float32) for r in res.results], axis=0)
```

```python
import contextlib
import os
import numpy as np
import concourse.bass as bass
import concourse.mybir as mybir
from concourse.bass_utils import run_bass_kernel_spmd

F32 = mybir.dt.float32
BF16 = mybir.dt.bfloat16
AF = mybir.ActivationFunctionType
ALU = mybir.AluOpType

ENGS = ("pe", "act", "dve", "pool", "sp")
DMA_K = 8

D = 1024
TT = 512
NB = TT // 128
IN_DIM = 2824
D_FF = 2816
NFF = D_FF // 128
EPS = 1e-6
BIG = 30000.0
SEQ = 4096
DEPTH = 2
NCORES = 8
STAGE = int(os.environ.get("KSTAGE", "99"))
SUB = int(os.environ.get("KSUB", "99"))


class Op:
    __slots__ = ("eng", "fn", "reads", "writes", "dma", "deps", "signal", "semval", "dslot", "gi")

    def __init__(self, eng, fn, reads, writes, dma):
        self.eng, self.fn, self.reads, self.writes, self.dma = eng, fn, reads, writes, dma
        self.deps = []
        self.signal = False
        self.semval = 0
        self.dslot = None


def _norm(keys):
    out = []
    for k in keys:
        if k is None:
            continue
        out.append(k if isinstance(k, tuple) else (k,))
    return out


class Prog:
    def __init__(self, nc):
        self.nc = nc
        self.ops = []
        self.state = {}
        self._cap = None

    def capture(self, body):
        assert self._cap is None
        self._cap = []
        body()
        lane, self._cap = self._cap, None
        return lane

    def replay(self, lane):
        for item in lane:
            self.add(*item)

    def add(self, eng, fn, reads=(), writes=(), dma=False):
        if self._cap is not None:
            self._cap.append((eng, fn, reads, writes, dma))
            return None
        op = Op(eng, fn, _norm(reads), _norm(writes), dma)
        op.gi = len(self.ops)
        deps = set()
        for k in op.reads:
            for kk, st in self._conf(k):
                if st[0] is not None:
                    deps.add((0, st[0]))
                if k[0].startswith("ps"):
                    for r in st[1]:
                        if self.ops[r].eng != eng:
                            deps.add((0, r))
        for k in op.writes:
            for kk, st in self._conf(k):
                if st[0] is not None:
                    deps.add((0, st[0]))
                for r in st[1]:
                    deps.add((1, r))
        final = set()
        for war, d in deps:
            if d == op.gi:
                continue
            a = self.ops[d]
            if a.eng == op.eng and not a.dma:
                if op.eng == "pe":
                    continue
            final.add(d)
        op.deps = sorted(final)
        for d in op.deps:
            self.ops[d].signal = True
        for k in op.reads:
            self._get(k)[1].append(op.gi)
        for k in op.writes:
            dd = self.state.setdefault(k[0], {})
            for kk in list(dd.keys()):
                if kk[: len(k)] == k and kk != k:
                    del dd[kk]
            dd[k] = [op.gi, []]
        self.ops.append(op)
        return op

    def _get(self, k):
        dd = self.state.setdefault(k[0], {})
        if k not in dd:
            par = None
            for kk, st in dd.items():
                if k[: len(kk)] == kk:
                    par = st
            dd[k] = [par[0] if par else None, []]
        return dd[k]

    def _conf(self, k):
        dd = self.state.get(k[0], {})
        for kk, st in dd.items():
            n = min(len(k), len(kk))
            if k[:n] == kk[:n]:
                yield kk, st

    def emit(self, final_wait_eng="sp"):
        nc = self.nc
        per = {e: [] for e in ENGS}
        for op in self.ops:
            per[op.eng].append(op)
        for e in ENGS:
            c = 0
            nd = 0
            for op in per[e]:
                if op.dma:
                    op.dslot = nd
                    nd += 1
                    op.signal = True
                elif op.signal:
                    c += 1
                    op.semval = c
        stack = contextlib.ExitStack()
        with stack:
            csem = {e: stack.enter_context(nc.semaphore("c_" + e)) for e in ENGS if e != "sp"}
            dsem = {e: [stack.enter_context(nc.semaphore("d_%s%d" % (e, i))) for i in range(DMA_K)]
                    for e in ("sp", "pool")}
            block = stack.enter_context(nc.Block())
            ops = self.ops

            def gen(e, engobj):
                waited = {}
                for op in per[e]:
                    need = {}
                    for d in op.deps:
                        a = ops[d]
                        if a.dma:
                            s = dsem[a.eng][a.dslot % DMA_K]
                            v = 16 * (a.dslot // DMA_K + 1)
                        else:
                            s = csem[a.eng]
                            v = a.semval
                        key = id(s)
                        if need.get(key, (None, 0))[1] < v:
                            need[key] = (s, v)
                    if op.dma and op.dslot >= DMA_K:
                        s = dsem[e][op.dslot % DMA_K]
                        v = 16 * (op.dslot // DMA_K)
                        key = id(s)
                        if need.get(key, (None, 0))[1] < v:
                            need[key] = (s, v)
                    for key, (s, v) in need.items():
                        if waited.get(key, 0) >= v:
                            continue
                        engobj.wait_ge(s, v)
                        waited[key] = v
                    ins = op.fn(engobj)
                    if op.dma:
                        ins.then_inc(dsem[e][op.dslot % DMA_K], 16)
                    elif op.signal:
                        ins.then_inc(csem[e], 1)
                if e == final_wait_eng:
                    for q in ("sp", "pool"):
                        n = len([o for o in per[q] if o.dma])
                        for i in range(min(DMA_K, n)):
                            cnt = (n - 1 - i) // DMA_K + 1
                            engobj.wait_ge(dsem[q][i], 16 * cnt)

            @block.tensor
            def _(t):
                gen("pe", t)

            @block.scalar
            def _(t):
                gen("act", t)

            @block.vector
            def _(t):
                gen("dve", t)

            @block.gpsimd
            def _(t):
                gen("pool", t)

            @block.sync
            def _(t):
                gen("sp", t)


def merge_lanes(lanes):
    lanes = [l for l in lanes if l]
    idx = [0] * len(lanes)
    out = []
    total = sum(len(l) for l in lanes)
    while len(out) < total:
        best = min((i for i in range(len(lanes)) if idx[i] < len(lanes[i])), key=lambda i: (idx[i] + 0.5) / len(lanes[i]))
        out.append(lanes[best][idx[best]])
        idx[best] += 1
    return out


def host_consts():
    ident = np.eye(128, dtype=np.float32)
    i = np.arange(128)
    triu = (i[:, None] <= i[None, :]).astype(np.float32)
    m1 = np.where(i[None, :] >= i[:, None], BIG, 0.0).astype(np.float32)
    m3 = np.where(i[None, :] < i[:, None], -BIG, 0.0).astype(np.float32)
    slopes = 2.0 ** (-8.0 * np.arange(1, 9) / 8.0)
    eb = np.zeros((128, 8, 256), np.float32)
    kk = i[:, None]
    qq = i[None, :]
    for h in range(8):
        relp = qq + 128 - kk
        eb[:, h, 0:128] = np.where(kk > qq, np.exp(-slopes[h] * relp), 0.0)
        relc = qq - kk
        eb[:, h, 128:256] = np.where(kk <= qq, np.exp(-slopes[h] * relc), 0.0)
    return {"c_ident": ident, "c_triu": triu, "c_m1": m1, "c_m3": m3, "c_eb": eb.reshape(128, 2048)}


def build_program(ntiles, depth, dbg=False):
    T = ntiles * TT
    nc = bass.Bass("TRN2", target_bir_lowering=False)
    din = lambda n, s: nc.dram_tensor(n, s, F32, kind="ExternalInput").ap()
    x_d = din("x", [T, D])
    attn_norm_d = din("attn_norm", [depth, D])
    w_in_d = din("w_in", [depth, D, IN_DIM])
    sinks_d = din("attn_sinks", [depth, 8])
    gconv_d = din("gdn_conv_w", [depth, 4, 1536])
    alog_d = din("gdn_a_log", [depth, 4])
    dtb_d = din("gdn_dt_bias", [depth, 4])
    gnorm_d = din("gdn_norm", [depth, 128])
    w_out_d = din("w_out", [depth, D, D])
    ffn_norm_d = din("ffn_norm", [depth, D])
    w_ffn_d = din("w_ffn_in", [depth, D, 2 * D_FF])
    fconv_d = din("ffn_conv_w", [depth, 3, D_FF])
    fbias_d = din("ffn_conv_b", [depth, D_FF])
    w_down_d = din("w_down", [depth, D_FF, D])
    fnorm_d = din("final_norm", [D])
    c_ident = din("c_ident", [128, 128])
    c_triu = din("c_triu", [128, 128])
    c_m1 = din("c_m1", [128, 128])
    c_m3 = din("c_m3", [128, 128])
    c_eb = din("c_eb", [128, 2048])
    out_d = nc.dram_tensor("out", [T, D], F32, kind="ExternalOutput").ap()
    dsc = lambda n, s_: nc.dram_tensor(n, s_, BF16, kind="Internal").ap()
    wsc_in = [dsc("wsc_in%d" % l, [D, IN_DIM]) for l in range(depth)]
    wsc_out = [dsc("wsc_out%d" % l, [D, D]) for l in range(depth)]
    wsc_ffn = [dsc("wsc_ffn%d" % l, [D, 2 * D_FF]) for l in range(depth)]
    wsc_dn = [dsc("wsc_dn%d" % l, [D_FF, D]) for l in range(depth)]
    dbg_d = {}
    if dbg:
        dbg_d["mixed"] = nc.dram_tensor("dbg_mixed", [D, TT], F32, kind="ExternalOutput").ap()
        dbg_d["oT"] = nc.dram_tensor("dbg_oT", [512, TT], F32, kind="ExternalOutput").ap()
        dbg_d["xmid"] = nc.dram_tensor("dbg_xmid", [D, TT], F32, kind="ExternalOutput").ap()
        dbg_d["xout"] = nc.dram_tensor("dbg_xout", [D, TT], F32, kind="ExternalOutput").ap()

    es = contextlib.ExitStack()
    with es, nc.allow_low_precision("bf16 matmul operands with fp32 PSUM accumulation"):
        sb = lambda n, s, d: es.enter_context(nc.sbuf_tensor(n, s, d))
        P = Prog(nc)

        def MM(out, lhsT, rhs, start, stop, r, w):
            P.add("pe", lambda e: e.matmul(out, lhsT, rhs, start=start, stop=stop), r, w)

        def TR(out, in_, ident, r, w):
            P.add("pe", lambda e: e.transpose(out, in_, ident), r, w)

        def ACTV(out, in_, func, r, w, bias=None, scale=None):
            kw = {}
            if bias is not None:
                kw["bias"] = bias
            if scale is not None:
                kw["scale"] = scale
            P.add("act", lambda e: e.activation(out, in_, func, **kw), r, w)

        def CP(eng, out, in_, r, w):
            if eng == "act":
                P.add("act", lambda e: e.activation(out, in_, AF.Copy), r, w)
            else:
                P.add(eng, lambda e: e.tensor_copy(out, in_), r, w)

        def TTOP(out, in0, in1, op, r, w, eng="dve"):
            P.add(eng, lambda e: e.tensor_tensor(out, in0, in1, op), r, w)

        def TS(out, in0, s1, op0, r, w, s2=None, op1=None, eng="dve"):
            if op1 is None:
                P.add(eng, lambda e: e.tensor_scalar(out, in0, s1, None, op0), r, w)
            else:
                P.add(eng, lambda e: e.tensor_scalar(out, in0, s1, s2, op0, op1), r, w)

        def STT(out, in0, scalar, in1, op0, op1, r, w, eng="dve"):
            P.add(eng, lambda e: e.scalar_tensor_tensor(out, in0, scalar, in1, op0, op1), r, w)

        def RECIP(out, in_, r, w):
            P.add("dve", lambda e: e.reciprocal(out, in_), r, w)

        def MSET(t, val, w, eng="dve"):
            P.add(eng, lambda e: e.memset(t, val), (), w)

        def DMA(eng, out, in_, r, w, slow=False):
            if slow:
                P.add(eng, lambda e: e.dma_start(out=out, in_=in_, allow_slow_non_contiguous=True), r, w, dma=True)
            else:
                P.add(eng, lambda e: e.dma_start(out=out, in_=in_), r, w, dma=True)

        idf = sb("idf", [128, 128], F32)
        idb = sb("idb", [128, 128], BF16)
        ones_f = sb("ones_f", [128, 128], F32)
        ones_b = sb("ones_b", [128, 128], BF16)
        triu = sb("triu", [128, 128], F32)
        m1 = sb("m1", [128, 128], F32)
        m3 = sb("m3", [128, 128], F32)
        eb = sb("eb", [128, 8, 256], BF16)
        onespad = sb("onespad", [128, 192], BF16)
        stg = sb("stg", [128, 128], F32)
        NPA = 87
        pvA = [sb("pvA%d" % l, [128, NPA], F32) for l in range(depth)]
        pvB = [sb("pvB%d" % l, [128, 66], F32) for l in range(depth)]
        pvF = sb("pvF", [128, 8], F32)
        srow = sb("srow", [1, 16], F32)
        bc = [sb("bc%d" % l, [128, 16], F32) for l in range(depth)]
        expA = [sb("expA%d" % l, [128, 4], F32) for l in range(depth)]
        es8 = [sb("es8%d" % l, [128, 4, 2], F32) for l in range(depth)]
        escol = [sb("escol%d" % l, [128, 4], F32) for l in range(depth)]
        diagw1 = sb("diagw", [128, 12, 4, 128], BF16)
        diagw = [diagw1 for l in range(depth)]
        kdT = [sb("kdT%d" % l, [128, 2, 128 + TT], BF16) for l in range(depth)]
        Vp = [sb("Vp%d" % l, [128, 1 + NB, 2, 192], BF16) for l in range(depth)]
        chalo = [sb("chalo%d" % l, [128, 12, 3], BF16) for l in range(depth)]
        fhalo = [sb("fhalo%d" % l, [128, NFF, 2], BF16) for l in range(depth)]
        S = [sb("S%d" % l, [128, 4, 128], F32) for l in range(depth)]
        Sb = [sb("Sb%d" % l, [128, 4, 128], BF16) for l in range(depth)]
        xT = sb("xT", [128, 8, TT], F32)
        sqb = [sb("sqb%d" % i, [128, TT], BF16) for i in range(2)]
        sqn = [sb("sqn%d" % i, [128, TT], BF16) for i in range(2)]
        rb2 = sb("rb2", [128, TT], F32)
        rstd = sb("rstd", [128, TT], F32)
        hT = sb("hT", [128, 8, TT], BF16)
        qT = sb("qT", [128, 4, TT], BF16)
        cT = sb("cT", [128, 12, 3 + TT], BF16)
        gz = sb("gz", [128, 4, TT], BF16)
        ba = sb("ba", [128, NB, 8], F32)
        wA = [sb("wA%d" % i, [128, 8, 256], BF16) for i in range(4)]
        wk = sb("wk", [128, 8, 256], BF16)
        wv = sb("wv", [128, 8, 136], BF16)
        wD = [sb("wD%d" % i, [128, NFF, 128], BF16) for i in range(2)]
        PT = [sb("PT%d" % i, [128, 256], BF16) for i in range(2)]
        den = sb("den", [128, 128], F32)
        sil = [sb("sil%d" % i, [128, TT], F32) for i in range(2)]
        rb = rstd
        knT = sb("knT", [128, 4, TT], BF16)
        ssk = sb("ssk", [128, NB, 4], F32)
        ktm = sb("ktm", [128, 4, 128], BF16)
        vtm = sb("vtm", [128, 4, 128], BF16)
        col2 = [{n: sb("col%d_" % i + n, [128, 4], F32) for n in
                 ("e1", "spb", "xa", "e2", "spa", "g", "G", "Gl", "Gp", "nG", "L1", "t1", "t2", "kb", "kd", "beta", "glast")}
                for i in range(2)]
        diagG = [sb("diagG%d" % i, [128, 128], F32) for i in range(2)]
        E1 = [sb("E1_%d" % i, [128, 128], F32) for i in range(2)]
        E3 = [sb("E3_%d" % i, [128, 128], F32) for i in range(2)]
        Eq = [sb("Eq_%d" % i, [128, 128], F32) for i in range(2)]
        Xb = [sb("Xb%d" % i, [128, 4, 128], BF16) for i in range(2)]
        XTb = [sb("XTb%d" % i, [128, 4, 128], BF16) for i in range(2)]
        Qb = [sb("Qb%d" % i, [128, 4, 128], BF16) for i in range(2)]
        dbl = lambda n: [sb("%s_%d" % (n, i), [128, 4, 128], BF16) for i in range(2)]
        X0, XT0, Q0 = dbl("X0"), dbl("XT0"), dbl("Q0")
        qkT4, qdT4, vb4, kb4, kd4 = dbl("qkT4"), dbl("qdT4"), dbl("vb4"), dbl("kb4"), dbl("kd4")
        nwT4 = sb("nwT4", [128, 4, 128], BF16)
        vnew4 = sb("vnew4", [128, 4, 128], BF16)
        IpA = dbl("IpA")
        ET4 = sb("ET4", [128, 4, 128], BF16)
        Yv4 = sb("Yv4", [128, 4, 128], BF16)
        Yk4 = sb("Yk4", [128, 4, 128], BF16)
        oT = sb("oT", [128, 4, TT], F32)
        gpre = [sb("gpre%d" % i, [128, 2 + TT], BF16) for i in range(2)]
        diagf = [sb("diagf%d" % i, [128, 3, 128], BF16) for i in range(2)]
        sg = [sb("sg%d" % i, [128, TT], BF16) for i in range(2)]
        big = sb("big", [128, NFF * TT], BF16)
        gT = big[:, :].rearrange("p (j t) -> p j t", j=NFF)
        bigf = big.bitcast(F32)
        xt = bigf[:, 0:NB * D].rearrange("p (t d) -> p t d", t=NB)
        ps = [es.enter_context(nc.psum_tensor("ps%d" % i, [128, 512], F32)) for i in range(8)]
        PSK = ["ps%d" % i for i in range(8)]

        DMA("sp", idf[:], c_ident, (), ["idf"])
        DMA("sp", triu[:], c_triu, (), ["triu"])
        DMA("sp", m1[:], c_m1, (), ["m1"])
        DMA("sp", m3[:], c_m3, (), ["m3"])
        DMA("pool", eb[:], c_eb.rearrange("p (h c) -> p h c", h=8), (), ["eb"])
        for l in range(depth):
            for (src, dst, nm, rows, rb_) in ((w_in_d[l], wsc_in[l], "wsc_in%d" % l, D, 256), (w_out_d[l], wsc_out[l], "wsc_out%d" % l, D, 512),
                                              (w_ffn_d[l], wsc_ffn[l], "wsc_ffn%d" % l, D, 128), (w_down_d[l], wsc_dn[l], "wsc_dn%d" % l, D_FF, 256)):
                for i, r0 in enumerate(range(0, rows, rb_)):
                    DMA("pool", dst[r0:r0 + rb_, :], src[r0:r0 + rb_, :], (), [(nm, i)])
        MSET(ones_f[:], 1.0, ["ones_f"])
        MSET(ones_b[:], 1.0, ["ones_b"])
        MSET(onespad[:], 0.0, ["onespad"])
        MSET(onespad[:, 64:128], 1.0, ["onespad"])
        CP("act", idb[:], idf[:], ["idf"], ["idb"])
        for l in range(depth):
            MSET(Vp[l][:], 0.0, ["Vp%d" % l])
            MSET(kdT[l][:], 0.0, ["kdT%d" % l])
            MSET(chalo[l][:], 0.0, ["chalo%d" % l])
            MSET(fhalo[l][:], 0.0, ["fhalo%d" % l])
            MSET(S[l][:], 0.0, ["S%d" % l])
            MSET(Sb[l][:], 0.0, ["Sb%d" % l])
        for l in range(depth):
            DMA("sp", stg[0:8, :], attn_norm_d[l].rearrange("(c p) -> c p", p=128), (), [("stg", 0)])
            DMA("sp", stg[8:16, :], ffn_norm_d[l].rearrange("(c p) -> c p", p=128), (), [("stg", 1)])
            DMA("sp", stg[16:38, :], fbias_d[l].rearrange("(c p) -> c p", p=128), (), [("stg", 2)])
            DMA("sp", stg[38:39, :], gnorm_d[l].rearrange("(c p) -> c p", p=128), (), [("stg", 3)])
            DMA("sp", stg[39:87, :], gconv_d[l].rearrange("t (j p) -> (t j) p", p=128), (), [("stg", 4)])
            TR(ps[0][:, 0:NPA], stg[0:NPA, :], idf[0:NPA, 0:NPA], ["stg", "idf"], [PSK[0]])
            CP("dve", pvA[l][:], ps[0][:, 0:NPA], [PSK[0]], ["pvA%d" % l])
            DMA("sp", stg[0:66, :], fconv_d[l].rearrange("t (j p) -> (t j) p", p=128), (), ["stg"])
            TR(ps[1][:, 0:66], stg[0:66, :], idf[0:66, 0:66], ["stg", "idf"], [PSK[1]])
            CP("dve", pvB[l][:], ps[1][:, 0:66], [PSK[1]], ["pvB%d" % l])
            DMA("sp", srow[0:1, 0:4], dtb_d[l].rearrange("(o c) -> o c", o=1), (), [("srow", 0)])
            DMA("sp", srow[0:1, 4:8], alog_d[l].rearrange("(o c) -> o c", o=1), (), [("srow", 1)])
            DMA("sp", srow[0:1, 8:16], sinks_d[l].rearrange("(o c) -> o c", o=1), (), [("srow", 2)])
            MM(ps[2][:, 0:16], ones_f[0:1, :], srow[0:1, :], True, True, ["ones_f", "srow"], [PSK[2]])
            CP("dve", bc[l][:], ps[2][:, 0:16], [PSK[2]], ["bc%d" % l])
            ACTV(expA[l][:], bc[l][:, 4:8], AF.Exp, ["bc%d" % l], ["expA%d" % l])
            ACTV(es8[l][:], bc[l][:, 8:16].rearrange("p (c two) -> p c two", two=2), AF.Exp, ["bc%d" % l], ["es8%d" % l])
            CP("dve", escol[l][0:64, :], es8[l][0:64, :, 0], ["es8%d" % l], [("escol%d" % l, 0)])
            CP("dve", escol[l][64:128, :], es8[l][64:128, :, 1], ["es8%d" % l], [("escol%d" % l, 1)])
        DMA("sp", stg[0:8, :], fnorm_d.rearrange("(c p) -> c p", p=128), (), ["stg"])
        TR(ps[0][:, 0:8], stg[0:8, :], idf[0:8, 0:8], ["stg", "idf"], [PSK[0]])
        CP("dve", pvF[:], ps[0][:, 0:8], [PSK[0]], ["pvF"])

        wstate = {"i": 0}

        def next_wA():
            i = wstate["i"] % len(wA)
            wstate["i"] += 1
            return wA[i], "wA%d" % i

        pjs = {"i": 0}

        def next_pj():
            i = pjs["i"] % 2
            pjs["i"] += 1
            return ps[i], PSK[i]

        def norm_accum(c):
            s = sqn[c % 2]
            ACTV(s[:], xT[:, c, :], AF.Square, [("xT", c)], ["sqn%d" % (c % 2)])
            MM(ps[6][:], ones_b[:], s[:], c == 0, c == 7, ["ones_b", "sqn%d" % (c % 2)], [PSK[6]])

        def norm_finish_rstd():
            ACTV(rstd[:], ps[6][:], AF.Ln, [PSK[6]], ["rstd"], bias=EPS, scale=1.0 / D)
            ACTV(rstd[:], rstd[:], AF.Exp, ["rstd"], ["rstd"], scale=-0.5)

        def rmsnorm_to_hT(wcol):
            norm_finish_rstd()
            for c in range(8):
                STT(hT[:, c, :], xT[:, c, :], wcol(c), rstd[:], ALU.mult, ALU.mult, [("xT", c), "rstd"], [("hT", c)])

        def proj_fm(wt, wkey, col0, nchunks, evac):
            for f in range(nchunks):
                pt, pk = next_pj()
                for k in range(8):
                    MM(pt[:], wt[:, k, col0 + f * 128: col0 + (f + 1) * 128], hT[:, k, :], k == 0, k == 7,
                       [wkey, ("hT", k)], [pk])
                evac(f, pt, pk)

        for it in range(ntiles):
            first = (it == 0)
            DMA("sp", xt, x_d[it * TT:(it + 1) * TT, :].rearrange("(t p) d -> p t d", p=128), (), ["big"])
            for c in range(8):
                pt, pk = next_pj()
                for t in range(NB):
                    TR(pt[:, t * 128:(t + 1) * 128], xt[:, t, c * 128:(c + 1) * 128], idf[:], ["big", "idf"], [pk])
                CP("dve", xT[:, c, :], pt[:], [pk], [("xT", c)])
                norm_accum(c)

            for l in range(depth):
                L = "%d" % l
                w_in3 = wsc_in[l].rearrange("(k p) n -> p k n", p=128)
                WIN = "wsc_in" + L
                rmsnorm_to_hT(lambda c: pvA[l][:, c:c + 1])
                for j in range(12):
                    for t in range(4):
                        TS(diagw1[:, j, t, :], idb[:], pvA[l][:, 39 + t * 12 + j: 40 + t * 12 + j], ALU.mult,
                           ["idb", "pvA" + L], [("diagw", j, t)])
                for g in range(2):
                    wt, wkey = next_wA()
                    DMA("sp", wt[:], w_in3[:, :, g * 256:(g + 1) * 256], [WIN], [wkey])
                    proj_fm(wt, wkey, 0, 2, lambda f, pt, pk, g=g: CP("act", qT[:, g * 2 + f, :], pt[:], [pk], [("qT", g * 2 + f)]))
                for j in range(2):
                    for h in range(2):
                        DMA("sp", wk[:, :, j * 128 + h * 64: j * 128 + h * 64 + 64], w_in3[:, :, 512 + j * 64: 576 + j * 64],
                            [WIN], [("wk", j, h)])
                proj_fm(wk, "wk", 0, 2,
                        lambda f, pt, pk: CP("dve", kdT[l][:, f, 128:128 + TT], pt[:], [pk], [("kdT" + L, f, 1)]))
                DMA("sp", wv[:, :, 0:128], w_in3[:, :, 640:768], [WIN], [("wv", 0)])
                DMA("sp", wv[:, :, 128:136], w_in3[:, :, 2816:2824], [WIN], [("wv", 1)])
                for t in range(NB):
                    pt, pk = next_pj()
                    for k in range(8):
                        MM(pt[:, 0:136], hT[:, k, t * 128:(t + 1) * 128], wv[:, k, :], k == 0, k == 7, ["wv", ("hT", k)], [pk])
                    CP("act", Vp[l][:, 1 + t, :, 64:128], pt[:, 0:128].rearrange("p (k d) -> p k d", k=2), [pk], [("Vp" + L, 1 + t)])
                    CP("dve", ba[:, t, :], pt[:, 128:136], [pk], [("ba", t)])
                CP("dve", cT[:, :, 0:3], chalo[l][:], ["chalo" + L], [("cT", "halo")])

                def gdn_proj(g6):
                    wt, wkey = next_wA()
                    DMA("sp", wt[:], w_in3[:, :, 768 + g6 * 256: 768 + (g6 + 1) * 256], [WIN], [wkey])
                    proj_fm(wt, wkey, 0, 2,
                            lambda f, pt, pk, g6=g6: CP("act" if f % 2 else "dve", cT[:, g6 * 2 + f, 3:3 + TT], pt[:], [pk],
                                                        [("cT", g6 * 2 + f)]))

                def fm_conv(idx):
                    hh = idx % 4
                    isk = idx >= 4
                    j = idx
                    pt, pk = (ps[5], PSK[5]) if idx % 2 == 0 else (ps[7], PSK[7])
                    for tap in range(4):
                        MM(pt[:], diagw1[:, j, tap, :], cT[:, j, tap:tap + TT], tap == 0, tap == 3,
                           [("diagw", j), ("cT", j), ("cT", "halo")], [pk])
                    sl = sil[idx % 2]
                    slk = "sil%d" % (idx % 2)
                    ACTV(sl[:], pt[:], AF.Silu, [pk], [slk])
                    sq_ = sqb[idx % 2]
                    sqk = "sqb%d" % (idx % 2)
                    ACTV(sq_[:], sl[:], AF.Square, [slk], [sqk])
                    ssb, ssbk = (ps[2], PSK[2]) if idx % 2 == 0 else (ps[4], PSK[4])
                    rbx, rbxk = (rstd, "rstd") if idx % 2 == 0 else (rb2, "rb2")
                    MM(ssb[:], ones_b[:], sq_[:], True, True, ["ones_b", sqk], [ssbk])
                    if isk:
                        for tb in range(NB):
                            MM(ps[3][:, tb * 4 + hh: tb * 4 + hh + 1], sq_[:, tb * 128:(tb + 1) * 128], ones_b[:, 0:1],
                               (hh == 0 and tb == 0), (hh == 3 and tb == NB - 1), [sqk, "ones_b"], [PSK[3]])
                    ACTV(rbx[:], ssb[:], AF.Ln, [ssbk], [rbxk], bias=EPS, scale=1.0)
                    ACTV(rbx[:], rbx[:], AF.Exp, [rbxk], [rbxk], scale=-0.5)
                    if isk:
                        TTOP(knT[:, hh, :], sl[:], rbx[:], ALU.mult, [slk, rbxk], [("knT", hh)])
                    else:
                        TTOP(hT[:, 4 + hh, :], sl[:], rbx[:], ALU.mult, [slk, rbxk], [("hT", 4 + hh)])

                for g6 in range(4):
                    gdn_proj(g6)

                def rest_of_inproj():
                    for g6 in range(4, 6):
                        gdn_proj(g6)
                    CP("dve", chalo[l][:], cT[:, :, TT:TT + 3], ["cT"], ["chalo" + L])
                    for g in range(2):
                        wt, wkey = next_wA()
                        DMA("sp", wt[:], w_in3[:, :, 2304 + g * 256: 2304 + (g + 1) * 256], [WIN], [wkey])
                        proj_fm(wt, wkey, 0, 2, lambda f, pt, pk, g=g: ACTV(gz[:, g * 2 + f, :], pt[:], AF.Silu, [pk], [("gz", g * 2 + f)]))

                lane_p = P.capture(rest_of_inproj)
                lane_c = P.capture(lambda: [fm_conv(i) for i in range(4, 8)])
                P.replay(merge_lanes([lane_p, lane_c]))

                for idx in range(4):
                    fm_conv(idx)
                CP("dve", ssk[:], ps[3][:, 0:NB * 4].rearrange("p (t h) -> p t h", h=4), [PSK[3]], ["ssk"])

                def attention_lane():
                    for tb in range(NB):
                        noprev = first and tb == 0
                        qs = slice(tb * 128, (tb + 1) * 128)
                        kprev = slice(tb * 128, (tb + 1) * 128)
                        kcur = slice(128 + tb * 128, 128 + (tb + 1) * 128)
                        for c in range(4):
                            j = c // 2
                            for par in range(2):
                                rows = slice(par * 64, par * 64 + 64)
                                sbk, skey = ps[2 + par], PSK[2 + par]
                                rk = [("kdT" + L, j), ("qT", c)]
                                if not noprev:
                                    MM(sbk[:, 0:128], kdT[l][rows, j, kprev], qT[rows, c, qs], True, False, rk, [skey])
                                    MM(sbk[:, 128:256], kdT[l][rows, j, kcur], qT[rows, c, qs], False, True, rk, [skey])
                                    ACTV(PT[par][:], sbk[:, 0:256], AF.Exp, [skey], ["PT%d" % par], scale=0.125)
                                    TTOP(PT[par][:], PT[par][:], eb[:, 2 * c + par, :], ALU.mult, ["PT%d" % par, "eb"], ["PT%d" % par])
                                else:
                                    MM(sbk[:, 128:256], kdT[l][rows, j, kcur], qT[rows, c, qs], True, True, rk, [skey])
                                    ACTV(PT[par][:, 128:256], sbk[:, 128:256], AF.Exp, [skey], ["PT%d" % par], scale=0.125)
                                    TTOP(PT[par][:, 128:256], PT[par][:, 128:256], eb[:, 2 * c + par, 128:256], ALU.mult,
                                         ["PT%d" % par, "eb"], ["PT%d" % par])
                            ob, okey = ps[4], PSK[4]
                            mms = []
                            for par in range(2):
                                vs = slice(64, 192) if par == 0 else slice(0, 128)
                                if not noprev:
                                    mms.append((0, Vp[l][:, tb, j, vs], PT[par][:, 0:128], par, ("Vp" + L, tb)))
                                mms.append((0, Vp[l][:, 1 + tb, j, vs], PT[par][:, 128:256], par, ("Vp" + L, 1 + tb)))
                            for par in range(2):
                                vs = slice(64, 192) if par == 0 else slice(0, 128)
                                if not noprev:
                                    mms.append((1, onespad[:, vs], PT[par][:, 0:128], par, "onespad"))
                                mms.append((1, onespad[:, vs], PT[par][:, 128:256], par, "onespad"))
                            for n, (reg, lt, rh, par, lkey) in enumerate(mms):
                                MM(ob[:, reg * 128:(reg + 1) * 128], lt, rh, n == 0, n == len(mms) - 1, [lkey, "PT%d" % par], [okey])
                            TS(den[:], ob[:, 128:256], escol[l][:, c:c + 1], ALU.add, [okey, "escol" + L], ["den"])
                            ACTV(den[:], den[:], AF.Ln, ["den"], ["den"])
                            ACTV(den[:], den[:], AF.Exp, ["den"], ["den"], scale=-1.0)
                            TTOP(hT[:, c, qs], ob[:, 0:128], den[:], ALU.mult, [okey, "den"], [("hT", c)])
                    CP("dve", kdT[l][:, :, 0:128], kdT[l][:, :, TT:TT + 128], ["kdT" + L], ["kdT" + L])
                    CP("dve", Vp[l][:, 0, :, :], Vp[l][:, NB, :, :], ["Vp" + L], ["Vp" + L])

                def stage_a(tb):
                    bp = tb % 2
                    B_ = "_%d" % bp
                    blk = slice(tb * 128, (tb + 1) * 128)
                    C = col2[bp]
                    ck = lambda n: "c%d_%s" % (bp, n)
                    for (base, dstt, dkey, bank) in ((4, ktm, "ktm", 0), (8, vtm, "vtm", 1)):
                        for hh in range(4):
                            for tap in range(4):
                                MM(ps[bank][:, hh * 128:(hh + 1) * 128], cT[:, base + hh, tb * 128 + tap: tb * 128 + tap + 128],
                                   diagw1[:, base + hh, tap, :], (hh == 0 and tap == 0), (hh == 3 and tap == 3),
                                   [("cT", base + hh), ("cT", "halo"), ("diagw", base + hh)], [PSK[bank]])
                        ACTV(dstt[:], ps[bank][:].rearrange("p (h d) -> p h d", h=4), AF.Silu, [PSK[bank]], [dkey])
                    ACTV(C["e1"][:], ba[:, tb, 0:4], AF.Exp, [("ba", tb)], [ck("e1")], scale=-1.0)
                    ACTV(C["spb"][:], C["e1"][:], AF.Ln, [ck("e1")], [ck("spb")], bias=1.0)
                    TTOP(C["xa"][:], ba[:, tb, 4:8], bc[l][:, 0:4], ALU.add, [("ba", tb), "bc" + L], [ck("xa")])
                    ACTV(C["e2"][:], C["xa"][:], AF.Exp, [ck("xa")], [ck("e2")])
                    ACTV(C["spa"][:], C["e2"][:], AF.Ln, [ck("e2")], [ck("spa")], bias=1.0)
                    STT(C["g"][:], C["spa"][:], -1.0, expA[l][:], ALU.mult, ALU.mult, [ck("spa"), "expA" + L], [ck("g")])
                    MM(ps[0][:, 0:4], triu[:], C["g"][:], True, False, ["triu", ck("g")], [PSK[0]])
                    MM(ps[0][:, 4:8], ones_f[:], C["g"][:], False, True, ["ones_f", ck("g")], [PSK[0]])
                    CP("dve", C["G"][:], ps[0][:, 0:4], [PSK[0]], [ck("G")])
                    CP("dve", C["Gl"][:], ps[0][:, 4:8], [PSK[0]], [ck("Gl")])
                    TTOP(C["Gp"][:], C["G"][:], C["spb"][:], ALU.subtract, [ck("G"), ck("spb")], [ck("Gp")])
                    TS(C["nG"][:], C["G"][:], -1.0, ALU.mult, [ck("G")], [ck("nG")])
                    ACTV(C["L1"][:], ssk[:, tb, :], AF.Ln, ["ssk"], [ck("L1")], bias=EPS)
                    STT(C["t1"][:], C["L1"][:], -0.5, C["Gp"][:], ALU.mult, ALU.add, [ck("L1"), ck("Gp")], [ck("t1")])
                    ACTV(C["kb"][:], C["t1"][:], AF.Exp, [ck("t1")], [ck("kb")])
                    TTOP(C["t2"][:], C["Gl"][:], C["G"][:], ALU.subtract, [ck("Gl"), ck("G")], [ck("t2")])
                    STT(C["t2"][:], C["L1"][:], -0.5, C["t2"][:], ALU.mult, ALU.add, [ck("L1"), ck("t2")], [ck("t2")])
                    ACTV(C["kd"][:], C["t2"][:], AF.Exp, [ck("t2")], [ck("kd")])
                    ACTV(C["beta"][:], C["spb"][:], AF.Exp, [ck("spb")], [ck("beta")], scale=-1.0)
                    ACTV(C["glast"][:], C["Gl"][:], AF.Exp, [ck("Gl")], [ck("glast")])
                    for hh in range(4):
                        TS(vb4[bp][:, hh, :], vtm[:, hh, :], C["beta"][:, hh:hh + 1], ALU.mult, ["vtm", ck("beta")], [("vb4" + B_, hh)])
                        TS(kb4[bp][:, hh, :], ktm[:, hh, :], C["kb"][:, hh:hh + 1], ALU.mult, ["ktm", ck("kb")], [("kb4" + B_, hh)])
                        TS(kd4[bp][:, hh, :], ktm[:, hh, :], C["kd"][:, hh:hh + 1], ALU.mult, ["ktm", ck("kd")], [("kd4" + B_, hh)])
                    for hh in range(4):
                        pr = hh % 2
                        rbk, rkey = ps[0], PSK[0]
                        kbk, kkey = ps[1], PSK[1]
                        dg, dgk = diagG[pr], "diagG%d" % pr
                        TS(dg[:], idf[:], C["G"][:, hh:hh + 1], ALU.mult, ["idf", ck("G")], [dgk])
                        MM(rbk[:, 0:128], ones_f[:], dg[:], True, False, ["ones_f", dgk], [rkey])
                        MM(rbk[:, 0:128], idf[:], m1[:], False, False, ["idf", "m1"], [rkey])
                        MM(rbk[:, 128:256], ones_f[:], dg[:], False, False, ["ones_f", dgk], [rkey])
                        MM(rbk[:, 128:256], idf[:], m3[:], False, False, ["idf", "m3"], [rkey])
                        MM(rbk[:, 256:384], ones_f[:], dg[:], False, True, ["ones_f", dgk], [rkey])
                        ACTV(E1[pr][:], rbk[:, 0:128], AF.Exp, [rkey, ck("Gp")], ["E1_%d" % pr], bias=C["Gp"][:, hh:hh + 1], scale=-1.0)
                        ACTV(E3[pr][:], rbk[:, 128:256], AF.Exp, [rkey, ck("nG")], ["E3_%d" % pr], bias=C["nG"][:, hh:hh + 1], scale=1.0)
                        ACTV(Eq[pr][:], rbk[:, 256:384], AF.Exp, [rkey], ["Eq_%d" % pr])
                        MM(kbk[:, 0:128], knT[:, hh, blk], knT[:, hh, blk], True, False, [("knT", hh)], [kkey])
                        MM(kbk[:, 128:256], knT[:, hh, blk], hT[:, 4 + hh, blk], False, True, [("knT", hh), ("hT", 4 + hh)], [kkey])
                        STT(X0[bp][:, hh, :], kbk[:, 0:128], -1.0, E1[pr][:], ALU.mult, ALU.mult, [kkey, "E1_%d" % pr], [("X0" + B_, hh)])
                        TTOP(IpA[bp][:, hh, :], idb[:], X0[bp][:, hh, :], ALU.subtract, ["idb", ("X0" + B_, hh)], [("IpA" + B_, hh)])
                        STT(qkT4[bp][:, hh, :], kbk[:, 128:256], 128.0 ** -0.5, E3[pr][:], ALU.mult, ALU.mult, [kkey, "E3_%d" % pr], [("qkT4" + B_, hh)])
                        STT(qdT4[bp][:, hh, :], hT[:, 4 + hh, blk], 128.0 ** -0.5, Eq[pr][:], ALU.mult, ALU.mult,
                            [("hT", 4 + hh), "Eq_%d" % pr], [("qdT4" + B_, hh)])
                    for hh in range(4):
                        MM(ps[1][:, hh * 128:(hh + 1) * 128], X0[bp][:, hh, :], idb[:], hh == 0, hh == 3, [("X0" + B_, hh), "idb"], [PSK[1]])
                    CP("act", XT0[bp][:], ps[1][:].rearrange("p (h d) -> p h d", h=4), [PSK[1]], ["XT0" + B_])
                    for hh in range(4):
                        TTOP(Q0[bp][:, hh, :], ps[1][:, hh * 128:(hh + 1) * 128], idf[:], ALU.add, [PSK[1], "idf"], [("Q0" + B_, hh)])

                def stage_b(tb):
                    bp = tb % 2
                    B_ = "_%d" % bp
                    blk = slice(tb * 128, (tb + 1) * 128)
                    C = col2[bp]
                    NLEV = 6
                    for lev in range(1, NLEV + 1):
                        p1 = lev % 2
                        if lev == 1:
                            Xs, XTs, Qs = X0[bp], XT0[bp], Q0[bp]
                            Xk, XTk, Qk = "X0" + B_, "XT0" + B_, "Q0" + B_
                        else:
                            p0 = (lev - 1) % 2
                            Xs, XTs, Qs = Xb[p0], XTb[p0], Qb[p0]
                            Xk, XTk, Qk = "Xb%d" % p0, "XTb%d" % p0, "Qb%d" % p0
                        for hh in range(4):
                            MM(ps[5][:, hh * 128:(hh + 1) * 128], XTs[:, hh, :], Xs[:, hh, :], hh == 0, hh == 3, [XTk, Xk], [PSK[5]])
                        if lev < NLEV:
                            for hh in range(4):
                                MM(ps[6][:, hh * 128:(hh + 1) * 128], Xs[:, hh, :], XTs[:, hh, :], hh == 0, hh == 3, [XTk, Xk], [PSK[6]])
                        CP("act", Xb[p1][:], ps[5][:].rearrange("p (h d) -> p h d", h=4), [PSK[5]], ["Xb%d" % p1])
                        if lev < NLEV:
                            CP("dve", XTb[p1][:], ps[6][:].rearrange("p (h d) -> p h d", h=4), [PSK[6]], ["XTb%d" % p1])
                        for hh in range(4):
                            MM(ps[7][:, hh * 128:(hh + 1) * 128], Xb[p1][:, hh, :], Qs[:, hh, :], hh == 0, hh == 3, ["Xb%d" % p1, Qk], [PSK[7]])
                        TTOP(Qb[p1][:], ps[7][:].rearrange("p (h d) -> p h d", h=4), Qs[:], ALU.add, [PSK[7], Qk], ["Qb%d" % p1])
                    TTt = Qb[NLEV % 2]
                    TTk = "Qb%d" % (NLEV % 2)
                    for hh in range(4):
                        MM(ps[5][:, hh * 128:(hh + 1) * 128], IpA[bp][:, hh, :], TTt[:, hh, :], hh == 0, hh == 3, [("IpA" + B_, hh), TTk], [PSK[5]])
                    for hh in range(4):
                        STT(ET4[:, hh, :], ps[5][:, hh * 128:(hh + 1) * 128], -1.0, idf[:], ALU.mult, ALU.add, [PSK[5], "idf"], [("ET4", hh)])
                    for hh in range(4):
                        MM(ps[6][:, hh * 128:(hh + 1) * 128], TTt[:, hh, :], vb4[bp][:, hh, :], hh == 0, hh == 3, [TTk, ("vb4" + B_, hh)], [PSK[6]])
                    for hh in range(4):
                        MM(ps[7][:, hh * 128:(hh + 1) * 128], TTt[:, hh, :], kb4[bp][:, hh, :], hh == 0, hh == 3, [TTk, ("kb4" + B_, hh)], [PSK[7]])
                    CP("act", Yv4[:], ps[6][:].rearrange("p (h d) -> p h d", h=4), [PSK[6]], ["Yv4"])
                    CP("dve", Yk4[:], ps[7][:].rearrange("p (h d) -> p h d", h=4), [PSK[7]], ["Yk4"])
                    for hh in range(4):
                        MM(ps[5][:, hh * 128:(hh + 1) * 128], kb4[bp][:, hh, :], TTt[:, hh, :], hh == 0, False, [("kb4" + B_, hh), TTk], [PSK[5]])
                        MM(ps[5][:, hh * 128:(hh + 1) * 128], Yk4[:, hh, :], ET4[:, hh, :], False, hh == 3, ["Yk4", ("ET4", hh)], [PSK[5]])
                    ACTV(nwT4[:], ps[5][:].rearrange("p (h d) -> p h d", h=4), AF.Copy, [PSK[5]], ["nwT4"], scale=-1.0)
                    for hh in range(4):
                        MM(ps[6][:, hh * 128:(hh + 1) * 128], TTt[:, hh, :], vb4[bp][:, hh, :], hh == 0, False, [TTk, ("vb4" + B_, hh)], [PSK[6]])
                        MM(ps[6][:, hh * 128:(hh + 1) * 128], ET4[:, hh, :], Yv4[:, hh, :], False, False, [("ET4", hh), "Yv4"], [PSK[6]])
                        MM(ps[6][:, hh * 128:(hh + 1) * 128], nwT4[:, hh, :], Sb[l][:, hh, :], False, hh == 3, ["nwT4", "Sb" + L], [PSK[6]])
                    CP("dve", vnew4[:], ps[6][:].rearrange("p (h d) -> p h d", h=4), [PSK[6]], ["vnew4"])
                    for hh in range(4):
                        MM(ps[7][:, hh * 128:(hh + 1) * 128], Sb[l][:, hh, :], qdT4[bp][:, hh, :], hh == 0, False, ["Sb" + L, ("qdT4" + B_, hh)], [PSK[7]])
                        MM(ps[7][:, hh * 128:(hh + 1) * 128], vnew4[:, hh, :], qkT4[bp][:, hh, :], False, hh == 3, ["vnew4", ("qkT4" + B_, hh)], [PSK[7]])
                    CP("act", oT[:, :, blk], ps[7][:].rearrange("p (h d) -> p h d", h=4), [PSK[7]], [("oT", tb)])
                    for hh in range(4):
                        MM(ps[5][:, hh * 128:(hh + 1) * 128], kd4[bp][:, hh, :], vnew4[:, hh, :], hh == 0, hh == 3, [("kd4" + B_, hh), "vnew4"], [PSK[5]])
                    for hh in range(4):
                        STT(S[l][:, hh, :], S[l][:, hh, :], C["glast"][:, hh:hh + 1], ps[5][:, hh * 128:(hh + 1) * 128], ALU.mult, ALU.add,
                            ["S" + L, "c%d_glast" % bp, PSK[5]], [("S" + L, hh)])
                    CP("act", Sb[l][:], S[l][:], ["S" + L], ["Sb" + L])

                lane_attn = P.capture(attention_lane)
                lanes_a = [P.capture(lambda tb=tb: stage_a(tb)) for tb in range(NB)]
                lanes_b = [P.capture(lambda tb=tb: stage_b(tb)) for tb in range(NB)]
                seq = list(lanes_a[0])
                for k in range(NB):
                    seq += merge_lanes([lanes_b[k]] + ([lanes_a[k + 1]] if k + 1 < NB else []))
                P.replay(merge_lanes([seq, lane_attn]))
                if dbg and it == 0 and l == 0:
                    DMA("sp", dbg_d["oT"].rearrange("(c p) t -> p c t", p=128), oT[:], ["oT"], ())
                for hh in range(4):
                    s = sqb[hh % 2]
                    sk = "sqb%d" % (hh % 2)
                    ACTV(s[:], oT[:, hh, :], AF.Square, ["oT"], [sk])
                    gsb, gsbk = (ps[2], PSK[2]) if hh % 2 == 0 else (ps[4], PSK[4])
                    grb, grbk = (rstd, "rstd") if hh % 2 == 0 else (rb2, "rb2")
                    MM(gsb[:], ones_b[:], s[:], True, True, ["ones_b", sk], [gsbk])
                    ACTV(grb[:], gsb[:], AF.Ln, [gsbk], [grbk], bias=EPS, scale=1.0 / 128)
                    ACTV(grb[:], grb[:], AF.Exp, [grbk], [grbk], scale=-0.5)
                    sl = sil[hh % 2]
                    slk = "sil%d" % (hh % 2)
                    STT(sl[:], oT[:, hh, :], pvA[l][:, 38:39], grb[:], ALU.mult, ALU.mult, ["oT", grbk, "pvA" + L], [slk])
                    TTOP(hT[:, 4 + hh, :], sl[:], gz[:, hh, :], ALU.mult, [slk, ("gz", hh)], [("hT", 4 + hh)])
                if dbg and it == 0 and l == 0:
                    for c in range(8):
                        CP("act", sil[c % 2][:], hT[:, c, :], [("hT", c)], ["sil%d" % (c % 2)])
                        DMA("sp", dbg_d["mixed"][c * 128:(c + 1) * 128, :], sil[c % 2][:], ["sil%d" % (c % 2)], ())

                w_out3 = wsc_out[l].rearrange("(k p) n -> p k n", p=128)
                for grp in range(4):
                    wt, wkey = next_wA()
                    DMA("sp", wt[:], w_out3[:, :, grp * 256:(grp + 1) * 256], ["wsc_out" + L], [wkey])
                    proj_fm(wt, wkey, 0, 2,
                            lambda f, pt, pk, grp=grp: (TTOP(xT[:, grp * 2 + f, :], xT[:, grp * 2 + f, :], pt[:], ALU.add,
                                                             [pk, ("xT", grp * 2 + f)], [("xT", grp * 2 + f)]),
                                                        norm_accum(grp * 2 + f)))
                if dbg and it == 0 and l == 0:
                    DMA("sp", dbg_d["xmid"].rearrange("(c p) t -> p c t", p=128), xT[:], ["xT"], ())

                rmsnorm_to_hT(lambda c: pvA[l][:, 8 + c:9 + c])
                w_ffn3 = wsc_ffn[l].rearrange("(k p) n -> p k n", p=128)
                wtiles = {}

                def ffn_front(j):
                    if j % 2 == 0:
                        nj = min(2, NFF - j)
                        wg, wgk = next_wA()
                        DMA("sp", wg[:, :, 0:nj * 128], w_ffn3[:, :, j * 128:(j + nj) * 128], ["wsc_ffn" + L], [wgk])
                        wu, wuk = next_wA()
                        DMA("sp", wu[:, :, 0:nj * 128], w_ffn3[:, :, D_FF + j * 128: D_FF + (j + nj) * 128], ["wsc_ffn" + L], [wuk])
                        wtiles["cur"] = (wg, wgk, wu, wuk)
                    wg, wgk, wu, wuk = wtiles["cur"]
                    jj = j % 2
                    r2 = j % 2
                    gp, gpk = gpre[r2], "gpre%d" % r2
                    df, dfk = diagf[r2], "diagf%d" % r2
                    CP("dve", gp[:, 0:2], fhalo[l][:, j, :], [("fhalo" + L, j)], [(gpk, "h")])
                    pt, pk = ps[r2], PSK[r2]
                    for k in range(8):
                        MM(pt[:], wg[:, k, jj * 128:(jj + 1) * 128], hT[:, k, :], k == 0, k == 7, [wgk, ("hT", k)], [pk])
                    CP("act", gp[:, 2:2 + TT], pt[:], [pk], [(gpk, "b")])
                    CP("dve", fhalo[l][:, j, :], gp[:, TT:TT + 2], [(gpk, "b")], [("fhalo" + L, j)])
                    for tap in range(3):
                        TS(df[:, tap, :], idb[:], pvB[l][:, tap * NFF + j: tap * NFF + j + 1], ALU.mult, ["idb", "pvB" + L], [(dfk, tap)])
                    pu, puk = ps[3 + r2], PSK[3 + r2]
                    for k in range(8):
                        MM(pu[:], wu[:, k, jj * 128:(jj + 1) * 128], hT[:, k, :], k == 0, k == 7, [wuk, ("hT", k)], [puk])

                def ffn_back(j):
                    r2 = j % 2
                    gp, gpk = gpre[r2], "gpre%d" % r2
                    df, dfk = diagf[r2], "diagf%d" % r2
                    pc, pck = (ps[2], PSK[2]) if r2 == 0 else (ps[5], PSK[5])
                    pu, puk = ps[3 + r2], PSK[3 + r2]
                    for tap in range(3):
                        MM(pc[:], df[:, tap, :], gp[:, tap:tap + TT], tap == 0, tap == 2, [dfk, gpk], [pck])
                    ACTV(sg[r2][:], pc[:], AF.Silu, [pck, "pvA" + L], ["sg%d" % r2], bias=pvA[l][:, 16 + j:17 + j])
                    TTOP(gT[:, j, :], pu[:], sg[r2][:], ALU.mult, [puk, "sg%d" % r2], [("big", j)])

                for j in range(NFF + 1):
                    if j < NFF:
                        ffn_front(j)
                    if j >= 1:
                        ffn_back(j - 1)
                w_dn3 = wsc_dn[l].rearrange("(k p) n -> p k n", p=128)
                for grp in range(8):
                    wd_, wdk = wD[grp % 2], "wD%d" % (grp % 2)
                    DMA("sp", wd_[:], w_dn3[:, :, grp * 128:(grp + 1) * 128], ["wsc_dn" + L], [wdk])
                    for mmi in range(1):
                        m = grp
                        pt, pk = next_pj()
                        for j in range(NFF):
                            MM(pt[:], wd_[:, j, mmi * 128:(mmi + 1) * 128], gT[:, j, :], j == 0, j == NFF - 1, [wdk, ("big", j)], [pk])
                        TTOP(xT[:, m, :], xT[:, m, :], pt[:], ALU.add, [pk, ("xT", m)], [("xT", m)])
                        norm_accum(m)
                if dbg and it == 0 and l == 0:
                    DMA("sp", dbg_d["xout"].rearrange("(c p) t -> p c t", p=128), xT[:], ["xT"], ())

            norm_finish_rstd()
            for c in range(8):
                hf, hfk = sil[c % 2], "sil%d" % (c % 2)
                STT(hf[:], xT[:, c, :], pvF[:, c:c + 1], rstd[:], ALU.mult, ALU.mult, [("xT", c), "rstd", "pvF"], [hfk])
                pt, pk = next_pj()
                for t in range(NB):
                    TR(pt[:, t * 128:(t + 1) * 128], hf[:, t * 128:(t + 1) * 128], idf[:], [hfk, "idf"], [pk])
                CP("act" if c % 2 else "dve", xt[:, :, c * 128:(c + 1) * 128], pt[:].rearrange("p (t d) -> p t d", t=NB), [pk], ["big"])
            DMA("sp", out_d[it * TT:(it + 1) * TT, :].rearrange("(t p) d -> p t d", p=128), xt, ["big"], ())
        P.emit()
    return nc


_PARAM_NAMES = ("attn_norm", "w_in", "attn_sinks", "gdn_conv_w", "gdn_a_log", "gdn_dt_bias", "gdn_norm", "w_out",
                "ffn_norm", "w_ffn_in", "ffn_conv_w", "ffn_conv_b", "w_down", "final_norm")


def kernel(**inputs):
    x = np.ascontiguousarray(np.asarray(inputs["x"], dtype=np.float32))
    B = x.shape[0]
    params = {n: np.ascontiguousarray(np.asarray(inputs[n], dtype=np.float32)) for n in _PARAM_NAMES}
    consts = host_consts()
    nc = build_program(SEQ // TT, DEPTH)
    in_maps = []
    for b in range(B):
        m = {"x": x[b]}
        m.update(params)
        m.update(consts)
        in_maps.append(m)
    res = run_bass_kernel_spmd(nc, in_maps, core_ids=list(range(B)))
    return np.stack([np.asarray(r["out"], dtype=np.float32) for r in res.results], axis=0)
```
